# Optimizing a Trainium2 kernel written in Bass

```python
import math
import jax, jax.numpy as jnp
from jax import lax
import numpy as np

D_MODEL = 2048
BATCH = 8
SEQ = 4096
DEPTH = 4
DEC_BATCH = 8
DEC_SEQ = 2048
PAST_LEN = 128

GRID_W = 64
BLOCK = 128
EPS = 1e-6
ROPE_THETA = 10000.0
NEG = -1e30
MLA_HEADS = 8
MLA_NOPE = 128
MLA_ROPE = 64
MLA_QK = MLA_NOPE + MLA_ROPE
MLA_V = 128
Q_LORA = 512
KV_LORA = 256
MLA_W = MLA_HEADS * MLA_V
AX_HEADS = 8
AX_KV = 2
AX_HD = 128
AX_W = AX_HEADS * AX_HD
WIN_HEADS = 8
WIN_KV = 2
WIN_HD = 128
WINDOW = 128
WIN_W = WIN_HEADS * WIN_HD
N_BUCKETS = 32
MAX_DIST = 128
D_IN = (Q_LORA + KV_LORA + MLA_ROPE + MLA_W
        + AX_W + 2 * AX_KV * AX_HD + AX_W
        + WIN_W + 2 * WIN_KV * WIN_HD + WIN_W
        + 3 * D_MODEL)

kernel_name = 'hybrid_gated_mla_axial_window_encoder'


def _split_points():
    sizes = [Q_LORA, KV_LORA, MLA_ROPE, MLA_W,
             AX_W, AX_KV * AX_HD, AX_KV * AX_HD, AX_W,
             WIN_W, WIN_KV * WIN_HD, WIN_KV * WIN_HD, WIN_W,
             D_MODEL, D_MODEL, D_MODEL]
    pts, acc = [], 0
    for s in sizes[:-1]:
        acc += s
        pts.append(acc)
    return pts


def rmsnorm(x, g):
    xf = x.astype(jnp.float32)
    y = xf * lax.rsqrt(jnp.mean(xf * xf, axis=-1, keepdims=True) + EPS)
    return (y * g.astype(jnp.float32)).astype(x.dtype)


def rope(x, pos):
    half = x.shape[-1] // 2
    freqs = ROPE_THETA ** (-jnp.arange(half, dtype=jnp.float32) / half)
    ang = pos[:, None] * freqs[None, :]
    cos = jnp.cos(ang)[None, :, None, :]
    sin = jnp.sin(ang)[None, :, None, :]
    xf = x.astype(jnp.float32)
    x1, x2 = xf[..., :half], xf[..., half:]
    return jnp.concatenate([x1 * cos - x2 * sin, x2 * cos + x1 * sin], axis=-1).astype(x.dtype)


def dense_block_attn(q, k, v, n_kv):
    B, S, H, Dk = q.shape
    G = H // n_kv
    nb = S // BLOCK
    Dv = v.shape[-1]
    scale = Dk ** -0.5
    qb = q.reshape(B, nb, BLOCK, n_kv, G, Dk).transpose(1, 0, 2, 3, 4, 5)

    def one(qblk):
        s = jnp.einsum('bqkgd,bskd->bkgqs', qblk, k).astype(jnp.float32) * scale
        p = jax.nn.softmax(s, axis=-1).astype(v.dtype)
        return jnp.einsum('bkgqs,bskd->bqkgd', p, v)

    o = lax.map(one, qb)
    return o.transpose(1, 0, 2, 3, 4, 5).reshape(B, S, H * Dv)


def t5_bucket(rel):
    half = N_BUCKETS // 2
    max_exact = half // 2
    ret = jnp.where(rel > 0, half, 0)
    n = jnp.abs(rel)
    nf = jnp.maximum(n, 1).astype(jnp.float32)
    large = max_exact + (jnp.log(nf / max_exact) / math.log(MAX_DIST / max_exact)
                         * (half - max_exact)).astype(jnp.int32)
    large = jnp.minimum(large, half - 1)
    return ret + jnp.where(n < max_exact, n, large)


def window_attn(q, k, v, sink, rel_bias):
    B, S, H, D = q.shape
    KV = k.shape[2]
    G = H // KV
    nb = S // BLOCK
    span = BLOCK + 2 * WINDOW
    qb = q.reshape(B, nb, BLOCK, KV, G, D)
    pad = ((0, 0), (WINDOW, WINDOW), (0, 0), (0, 0))
    kp = jnp.pad(k, pad)
    vp = jnp.pad(v, pad)
    idx = (jnp.arange(nb) * BLOCK)[:, None] + jnp.arange(span)[None, :]
    kb = kp[:, idx]
    vb = vp[:, idx]
    s = jnp.einsum('bnqkgd,bnskd->bnkgqs', qb, kb).astype(jnp.float32) * (D ** -0.5)
    rel = jnp.arange(span)[None, :] - WINDOW - jnp.arange(BLOCK)[:, None]
    bias = rel_bias[t5_bucket(rel)].astype(jnp.float32)
    bias = bias.transpose(2, 0, 1).reshape(KV, G, BLOCK, span)
    kpos = idx - WINDOW
    valid = (jnp.abs(rel) <= WINDOW)[None] & ((kpos >= 0) & (kpos < S))[:, None, :]
    s = jnp.where(valid[None, :, None, None], s + bias, NEG)
    sk = jnp.broadcast_to(sink.astype(jnp.float32).reshape(KV, G, 1, 1), (B, nb, KV, G, BLOCK, 1))
    p = jax.nn.softmax(jnp.concatenate([s, sk], axis=-1), axis=-1)[..., :span].astype(v.dtype)
    o = jnp.einsum('bnkgqs,bnskd->bnqkgd', p, vb)
    return o.reshape(B, S, H * D)


def layer(x, c, ln_g, w_ada, b_ada, w_in, q_a_norm, w_q_up, kv_a_norm, w_kv_up,
          mla_qn, mla_kn, ax_qn, ax_kn, win_qn, win_kn, win_sink, rel_bias,
          w_br_mla, w_br_ax, w_br_win, w_out):
    B, S, _ = x.shape
    rows = S // GRID_W
    mod = c @ w_ada + b_ada
    shift = mod[:, None, :D_MODEL]
    scale = mod[:, None, D_MODEL:2 * D_MODEL]
    gate = mod[:, None, 2 * D_MODEL:]
    h = rmsnorm(x, ln_g) * (1.0 + scale) + shift
    proj = h @ w_in
    (cq, ckv, kr, g_mla, aq, ak, av, g_ax, wq, wk, wv, g_win,
     m_mla, m_ax, m_win) = jnp.split(proj, _split_points(), axis=-1)
    pos = jnp.arange(S, dtype=jnp.float32)
    row = jnp.repeat(jnp.arange(rows, dtype=jnp.float32), GRID_W)
    col = jnp.tile(jnp.arange(GRID_W, dtype=jnp.float32), rows)

    q = (rmsnorm(cq, q_a_norm) @ w_q_up).reshape(B, S, MLA_HEADS, MLA_QK)
    kv = (rmsnorm(ckv, kv_a_norm) @ w_kv_up).reshape(B, S, MLA_HEADS, MLA_NOPE + MLA_V)
    k_nope, v_mla = kv[..., :MLA_NOPE], kv[..., MLA_NOPE:]
    k_pe = jnp.broadcast_to(kr[:, :, None, :], (B, S, MLA_HEADS, MLA_ROPE))
    k = jnp.concatenate([k_nope, k_pe], axis=-1)
    q = rmsnorm(q, mla_qn)
    k = rmsnorm(k, mla_kn)
    q = jnp.concatenate([q[..., :MLA_NOPE], rope(q[..., MLA_NOPE:], pos)], axis=-1)
    k = jnp.concatenate([k[..., :MLA_NOPE], rope(k[..., MLA_NOPE:], pos)], axis=-1)
    o_mla = dense_block_attn(q, k, v_mla, MLA_HEADS)

    hd2 = AX_HD // 2
    aq = rmsnorm(aq.reshape(B, S, AX_HEADS, AX_HD), ax_qn)
    ak = rmsnorm(ak.reshape(B, S, AX_KV, AX_HD), ax_kn)
    av = av.reshape(B, S, AX_KV, AX_HD)
    aq = jnp.concatenate([rope(aq[..., :hd2], row), rope(aq[..., hd2:], col)], axis=-1)
    ak = jnp.concatenate([rope(ak[..., :hd2], row), rope(ak[..., hd2:], col)], axis=-1)
    o_ax = dense_block_attn(aq, ak, av, AX_KV)

    wq = rmsnorm(wq.reshape(B, S, WIN_HEADS, WIN_HD), win_qn)
    wk = rmsnorm(wk.reshape(B, S, WIN_KV, WIN_HD), win_kn)
    wv = wv.reshape(B, S, WIN_KV, WIN_HD)
    o_win = window_attn(wq, wk, wv, win_sink, rel_bias)

    y = (jax.nn.sigmoid(m_mla) * ((o_mla * jax.nn.silu(g_mla)) @ w_br_mla)
         + jax.nn.sigmoid(m_ax) * ((o_ax * jax.nn.silu(g_ax)) @ w_br_ax)
         + jax.nn.sigmoid(m_win) * ((o_win * jax.nn.silu(g_win)) @ w_br_win))
    return x + gate * (y @ w_out)


def setup_inputs(seed: int = 0) -> dict:
    key = jax.random.key(seed)
    ks = jax.random.split(key, 24)
    f32 = jnp.float32

    def nrm(k, shape, s):
        return jax.random.normal(k, shape, f32) * s

    def gain(k, shape):
        return 1.0 + 0.02 * jax.random.normal(k, shape, f32)

    return {
        'x_prompt': nrm(ks[0], (BATCH, SEQ, D_MODEL), 1.0),
        'x_sample': nrm(ks[1], (DEC_BATCH, DEC_SEQ, D_MODEL), 1.0),
        'c_prompt': nrm(ks[2], (BATCH, D_MODEL), 1.0),
        'c_sample': nrm(ks[3], (DEC_BATCH, D_MODEL), 1.0),
        'ln_g': gain(ks[4], (DEPTH, D_MODEL)),
        'w_ada': nrm(ks[5], (DEPTH, D_MODEL, 3 * D_MODEL), 0.5 * D_MODEL ** -0.5),
        'b_ada': nrm(ks[6], (DEPTH, 3 * D_MODEL), 0.01),
        'w_in': nrm(ks[7], (DEPTH, D_MODEL, D_IN), D_MODEL ** -0.5),
        'q_a_norm': gain(ks[8], (DEPTH, Q_LORA)),
        'w_q_up': nrm(ks[9], (DEPTH, Q_LORA, MLA_HEADS * MLA_QK), Q_LORA ** -0.5),
        'kv_a_norm': gain(ks[10], (DEPTH, KV_LORA)),
        'w_kv_up': nrm(ks[11], (DEPTH, KV_LORA, MLA_HEADS * (MLA_NOPE + MLA_V)), KV_LORA ** -0.5),
        'mla_qn': gain(ks[12], (DEPTH, MLA_QK)),
        'mla_kn': gain(ks[13], (DEPTH, MLA_QK)),
        'ax_qn': gain(ks[14], (DEPTH, AX_HD)),
        'ax_kn': gain(ks[15], (DEPTH, AX_HD)),
        'win_qn': gain(ks[16], (DEPTH, WIN_HD)),
        'win_kn': gain(ks[17], (DEPTH, WIN_HD)),
        'win_sink': nrm(ks[18], (DEPTH, WIN_HEADS), 1.0),
        'rel_bias': nrm(ks[19], (N_BUCKETS, WIN_HEADS), 0.5),
        'w_br_mla': nrm(ks[20], (DEPTH, MLA_W, D_MODEL), MLA_W ** -0.5),
        'w_br_ax': nrm(ks[21], (DEPTH, AX_W, D_MODEL), AX_W ** -0.5),
        'w_br_win': nrm(ks[22], (DEPTH, WIN_W, D_MODEL), WIN_W ** -0.5),
        'w_out': nrm(ks[23], (DEPTH, D_MODEL, D_MODEL), D_MODEL ** -0.5),
    }


def reference(x_prompt, x_sample, c_prompt, c_sample, ln_g, w_ada, b_ada, w_in,
              q_a_norm, w_q_up, kv_a_norm, w_kv_up, mla_qn, mla_kn, ax_qn, ax_kn,
              win_qn, win_kn, win_sink, rel_bias, w_br_mla, w_br_ax, w_br_win, w_out):
    y_prompt = x_prompt
    y_sample = x_sample
    for l in range(DEPTH):
        p = (ln_g[l], w_ada[l], b_ada[l], w_in[l], q_a_norm[l], w_q_up[l],
             kv_a_norm[l], w_kv_up[l], mla_qn[l], mla_kn[l], ax_qn[l], ax_kn[l],
             win_qn[l], win_kn[l], win_sink[l], rel_bias,
             w_br_mla[l], w_br_ax[l], w_br_win[l], w_out[l])
        y_prompt = layer(y_prompt, c_prompt, *p)
        y_sample = layer(y_sample, c_sample, *p)
    return (y_prompt, y_sample)
```

```python
import numpy as np
from contextlib import ExitStack
from collections import deque
import concourse.bass as bass
import concourse.mybir as mybir
from concourse.bass_utils import run_bass_kernel_spmd

F32 = mybir.dt.float32
BF16 = mybir.dt.bfloat16
AF = mybir.ActivationFunctionType
ALU = mybir.AluOpType
D = 2048
DIN = 13120
EPS = 1e-6
ENGS = ("pe", "act", "dve", "pool", "sp")
SAME_SYNC = True
NDS = 56
NEGM = -30000.0

C_CQ, C_CKV, C_KR, C_GM = 0, 512, 768, 832
C_AQ, C_AK, C_AV, C_GA = 1856, 2880, 3136, 3392
C_WQ, C_WK, C_WV, C_GW = 4416, 5440, 5696, 5952
C_MM, C_MA, C_MW = 6976, 9024, 11072


class Ev:
    __slots__ = ("key", "val")

    def __init__(self, key, val):
        self.key = key
        self.val = val


class Buf:
    def __init__(self, t, name, dsem=None):
        self.t = t
        self.name = name
        self.w = None
        self.r = []
        self.dsem = dsem

    def __getitem__(self, k):
        return self.t[k]


class KB:
    def __init__(self, nc, es, L, SP, SS):
        self.nc = nc
        self.L, self.SP, self.SS = L, SP, SS
        self.T = SP + SS
        self.es = es
        self.esem = {e: es.enter_context(nc.semaphore("s_" + e)) for e in ENGS}
        self.ecnt = {e: 0 for e in ENGS}
        self.dpool = [es.enter_context(nc.semaphore("d%d" % i)) for i in range(NDS)]
        self.dcnt = [0] * NDS
        self.dfree = list(range(NDS))
        self.q = {e: [] for e in ENGS}
        self.waited = {e: {} for e in ENGS}
        self.pending = {e: [] for e in ENGS}
        self.phase_dma = {e: {} for e in ENGS}
        self.pstack = None
        self.phase_sems = []
        self.nops = 0

    def sb(self, name, shape, dt, dma=False, persist=False):
        st = self.es if persist else self.pstack
        self.uid = getattr(self, "uid", 0) + 1
        name = "%s_%d" % (name, self.uid)
        t = st.enter_context(self.nc.sbuf_tensor(name, list(shape), dt))
        ds = None
        if dma:
            ds = self.dfree.pop()
            if not persist:
                self.phase_sems.append(ds)
        return Buf(t, name, ds)

    def pb(self, name, shape, dt=F32):
        self.uid = getattr(self, "uid", 0) + 1
        name = "%s_%d" % (name, self.uid)
        t = self.pstack.enter_context(self.nc.psum_tensor(name, list(shape), dt))
        return Buf(t, name)

    def semof(self, key):
        if key[0] == "e":
            return self.esem[key[2:]]
        return self.dpool[int(key[2:])]

    def _waits(self, eng, reads, writes):
        waits = {}

        def need(evt):
            if evt is None:
                return
            ev, e = evt
            if e == eng and (eng == "pe" or not SAME_SYNC):
                return
            assert ev.val is not None, "wait on unsignalled op"
            if self.waited[eng].get(ev.key, 0) >= ev.val:
                return
            waits[ev.key] = max(waits.get(ev.key, 0), ev.val)

        for b in reads:
            need(b.w)
        for b in writes:
            need(b.w)
            for r in b.r:
                need(r)
        for k, v in waits.items():
            self.waited[eng][k] = v
        return [(self.semof(k), v) for k, v in waits.items()]

    def op(self, eng, fn, reads=(), writes=(), sig=True):
        wl = self._waits(eng, reads, writes)
        if sig:
            self.ecnt[eng] += 1
            ev = Ev("e:" + eng, self.ecnt[eng])
            for p in self.pending[eng]:
                p.val = ev.val
            self.pending[eng] = []
        else:
            ev = Ev("e:" + eng, None)
            self.pending[eng].append(ev)
        semh = self.esem[eng]

        def run(e):
            for s, v in wl:
                e.wait_ge(s, v)
            ins = fn(e)
            if sig:
                ins.then_inc(semh, 1)

        self.q[eng].append(run)
        self.nops += 1
        for b in reads:
            b.r.append((ev, eng))
        for b in writes:
            b.w = (ev, eng)
            b.r = []
        return ev

    def dma(self, qeng, out_ap, in_ap, buf, load, slow=False):
        wl = self._waits(qeng, () if load else (buf,), (buf,) if load else ())
        di = buf.dsem
        self.dcnt[di] += 16
        ev = Ev("d:%d" % di, self.dcnt[di])
        semh = self.dpool[di]

        def run(e):
            for s, v in wl:
                e.wait_ge(s, v)
            e.dma_start(out=out_ap, in_=in_ap, allow_slow_non_contiguous=slow).then_inc(semh, 16)

        self.q[qeng].append(run)
        self.nops += 1
        if load:
            buf.w = (ev, "dma")
            buf.r = []
        else:
            buf.r.append((ev, "dma"))
        self.phase_dma[qeng][ev.key] = ev.val
        return ev

    def phase(self, name):
        kb = self

        class _P:
            def __enter__(s):
                kb.pstack = ExitStack()
                kb.pstack.__enter__()
                kb.phase_sems = []
                return s

            def __exit__(s, *a):
                if a[0] is None:
                    kb.flush(name)
                kb.pstack.__exit__(*a)
                kb.dfree.extend(kb.phase_sems)
                kb.pstack = None
                return False

        return _P()

    def flush(self, name):
        for eng in ENGS:
            assert not self.pending[eng], "unsignalled tail on " + eng
            wl = []
            for k, v in self.phase_dma[eng].items():
                if self.waited[eng].get(k, 0) < v:
                    self.waited[eng][k] = v
                    wl.append((self.semof(k), v))
            if wl:
                self.q[eng].append(lambda e, wl=wl: [e.wait_ge(s, v) for s, v in wl])
            self.phase_dma[eng] = {}
        q = self.q
        with self.nc.Block() as block:
            block.tensor(lambda e: [f(e) for f in q["pe"]])
            block.scalar(lambda e: [f(e) for f in q["act"]])
            block.vector(lambda e: [f(e) for f in q["dve"]])
            block.gpsimd(lambda e: [f(e) for f in q["pool"]])
            block.sync(lambda e: [f(e) for f in q["sp"]])
        self.q = {e: [] for e in ENGS}

    def mm(self, ob, out_ap, lhsT, rhs, reads, start=True, stop=True, sig=True):
        return self.op("pe", lambda e: e.matmul(out_ap, lhsT=lhsT, rhs=rhs, start=start, stop=stop,
                                               skip_group_check=True), reads=reads, writes=(ob,), sig=sig)

    def actf(self, ob, out_ap, in_ap, func, reads, scale=1.0, bias=0.0, accum=None, extra_w=()):
        kw = {}
        if accum is not None:
            kw["accum_out"] = accum
        return self.op("act", lambda e: e.activation(out=out_ap, in_=in_ap, func=func, bias=bias, scale=scale, **kw),
                       reads=reads, writes=(ob,) + tuple(extra_w))

    def tt(self, eng, ob, out_ap, a, b, opn, reads):
        return self.op(eng, lambda e: e.tensor_tensor(out=out_ap, in0=a, in1=b, op=opn), reads=reads, writes=(ob,))

    def stt(self, eng, ob, out_ap, a, sc, b, op0, op1, reads):
        return self.op(eng, lambda e: e.scalar_tensor_tensor(out=out_ap, in0=a, scalar=sc, in1=b, op0=op0, op1=op1),
                       reads=reads, writes=(ob,))

    def ts(self, eng, ob, out_ap, a, s1, s2, op0, op1, reads):
        if s2 is None:
            return self.op(eng, lambda e: e.tensor_scalar(out=out_ap, in0=a, scalar1=s1, scalar2=None, op0=op0),
                           reads=reads, writes=(ob,))
        return self.op(eng, lambda e: e.tensor_scalar(out=out_ap, in0=a, scalar1=s1, scalar2=s2, op0=op0, op1=op1),
                       reads=reads, writes=(ob,))

    def cp(self, eng, ob, out_ap, in_ap, reads):
        if eng == "act":
            return self.op("act", lambda e: e.copy(out=out_ap, in_=in_ap), reads=reads, writes=(ob,))
        return self.op(eng, lambda e: e.tensor_copy(out=out_ap, in_=in_ap), reads=reads, writes=(ob,))


def col(ap1d):
    return ap1d.rearrange("(p o) -> p o", o=1)


class Rot:
    def __init__(self, items):
        self.items = items
        self.i = 0

    def next(self):
        b = self.items[self.i % len(self.items)]
        self.i += 1
        return b


def build_program(L, SP, SS, taps=()):
    nc = bass.Bass("TRN2", target_bir_lowering=False)
    T = SP + SS
    NCH = T // 512

    def din(name, shape):
        return nc.dram_tensor(name, list(shape), F32, kind="ExternalInput").ap()

    I = {}
    I["x_p"] = din("x_p", [SP, D])
    I["x_s"] = din("x_s", [SS, D])
    I["c2"] = din("c2", [2, D])
    wshapes = dict(ln_g=[L, D], w_ada=[L, D, 3 * D], b_ada=[L, 3 * D], w_in=[L, D, DIN], q_a_norm=[L, 512],
                   w_q_up=[L, 512, 1536], kv_a_norm=[L, 256], w_kv_up=[L, 256, 2048], mla_qn=[L, 192],
                   mla_kn=[L, 192], ax_qn=[L, 128], ax_kn=[L, 128], win_qn=[L, 128], win_kn=[L, 128],
                   win_sink=[L, 8], rel_bias=[32, 8], w_br_mla=[L, 1024, D], w_br_ax=[L, 1024, D],
                   w_br_win=[L, 1024, D], w_out=[L, D, D])
    for k, s in wshapes.items():
        I[k] = din(k, s)
    I["k_ident"] = din("k_ident", [128, 128])
    I["k_rmat"] = din("k_rmat", [128, 128])
    I["k_cos_ax"] = din("k_cos_ax", [128, SP])
    I["k_sin_ax"] = din("k_sin_ax", [128, SP])
    I["k_cos_m"] = din("k_cos_m", [64, SP])
    I["k_sin_m"] = din("k_sin_m", [64, SP])
    I["k_widx"] = din("k_widx", [128, 384])
    I["k_wmask"] = din("k_wmask", [128, 384])
    y_p = nc.dram_tensor("y_p", [SP, D], F32, kind="ExternalOutput").ap()
    y_s = nc.dram_tensor("y_s", [SS, D], F32, kind="ExternalOutput").ap()

    def scr(name, shape, dt=BF16):
        kind = "ExternalOutput" if name in taps else "Internal"
        return nc.dram_tensor(name, list(shape), dt, kind=kind).ap()

    S = {}
    S["xs"] = scr("xs", [T, D], F32)
    S["modd"] = scr("modd", [L, 2, 3 * D], F32)
    S["btd"] = scr("btd", [128, 3 * 8 * 128], F32)
    S["hT"] = scr("hT", [D, T])
    S["cqnT"] = scr("cqnT", [512, T])
    S["ckvnT"] = scr("ckvnT", [256, T])
    S["krraw"] = scr("krraw", [64, T], F32)
    S["qnT"] = scr("qnT", [1024, T])
    S["qrT"] = scr("qrT", [512, T])
    S["knT"] = scr("knT", [1024, T])
    S["krT"] = scr("krT", [512, T])
    S["vm"] = scr("vm", [T, 1024])
    S["aqT"] = scr("aqT", [1024, T])
    S["akT"] = scr("akT", [256, T])
    S["wqT"] = scr("wqT", [1024, T])
    S["wkT"] = scr("wkT", [256, T])
    S["vaw"] = scr("vaw", [T, 512])
    S["gT"] = scr("gT", [3 * 1024, T])
    S["smT"] = scr("smT", [3 * 2048, T])
    S["AT"] = scr("AT", [3 * 1024, T])
    S["yT"] = scr("yT", [D, T])

    seqs = [(0, SP), (SP, SS)]

    def seq_of(t):
        return 0 if t < SP else 1

    def xsrc(l, t0, n):
        if l == 0:
            return I["x_p"][t0:t0 + n, :] if t0 < SP else I["x_s"][t0 - SP:t0 - SP + n, :]
        return S["xs"][t0:t0 + n, :]

    def xdst(l, t0, n):
        if l == L - 1:
            return y_p[t0:t0 + n, :] if t0 < SP else y_s[t0 - SP:t0 - SP + n, :]
        return S["xs"][t0:t0 + n, :]

    with ExitStack() as es:
        kb = KB(nc, es, L, SP, SS)
        ident = kb.sb("ident", [128, 128], BF16, dma=True, persist=True)
        rmat = kb.sb("rmat", [128, 128], BF16, dma=True, persist=True)
        ones = kb.sb("ones", [128, 128], BF16, persist=True)
        cT = kb.sb("cT", [128, 16, 128], BF16, persist=True)
        onesH = kb.sb("onesH", [128, 128], BF16, persist=True)
        gall = kb.sb("gall", [128, 16 * L], F32, dma=True, persist=True)
        rmatH = kb.sb("rmatH", [128, 128], BF16, persist=True)

        with kb.phase("init"):
            kb.dma("pool", ident[:], I["k_ident"], ident, True)
            kb.dma("pool", rmat[:], I["k_rmat"], rmat, True)
            kb.op("dve", lambda e: e.memset(ones[:], 1.0), writes=(ones,))
            kb.op("dve", lambda e: e.memset(onesH[:], 0.0), writes=(onesH,))
            kb.op("dve", lambda e: e.memset(onesH[0:64, :], 1.0), writes=(onesH,))
            kb.op("dve", lambda e: e.memset(rmatH[:], 0.0), writes=(rmatH,))
            kb.cp("dve", rmatH, rmatH[0:64, 0:64], rmat[0:64, 0:64], reads=(rmat,))
            kb.op("pool", lambda e: e.memset(gall[:], 1.0), writes=(gall,))
            for l_ in range(L):
                gspecs = [(I["q_a_norm"][l_, 0:128], 0, 128), (I["q_a_norm"][l_, 128:256], 1, 128),
                          (I["q_a_norm"][l_, 256:384], 2, 128), (I["q_a_norm"][l_, 384:512], 3, 128),
                          (I["kv_a_norm"][l_, 0:128], 4, 128), (I["kv_a_norm"][l_, 128:256], 5, 128),
                          (I["mla_qn"][l_, 0:128], 6, 128), (I["mla_qn"][l_, 128:192], 7, 64),
                          (I["mla_kn"][l_, 0:128], 8, 128), (I["mla_kn"][l_, 128:192], 9, 64),
                          (I["ax_qn"][l_, :], 10, 128), (I["ax_kn"][l_, :], 11, 128),
                          (I["win_qn"][l_, :], 12, 128), (I["win_kn"][l_, :], 13, 128)]
                for (src, c_, rows) in gspecs:
                    kb.dma("sp", gall[0:rows, 16 * l_ + c_:16 * l_ + c_ + 1], col(src), gall, True)
            rbb = kb.sb("rbb", [128, 256], F32, dma=True)
            widx = kb.sb("widx", [128, 384], F32, dma=True)
            wmask = kb.sb("wmask", [128, 384], F32, dma=True)
            bt = kb.sb("bt", [128, 3, 8, 128], F32, dma=True)
            tmpb = [kb.sb("tmpb%d" % i, [128, 3, 128], F32) for i in range(2)]
            kb.dma("sp", rbb[:], I["rel_bias"].rearrange("b h -> (b h)").partition_broadcast(128), rbb, True)
            kb.dma("sp", widx[:], I["k_widx"], widx, True)
            kb.dma("sp", wmask[:], I["k_wmask"], wmask, True)
            idx3 = widx[:].rearrange("p (d q) -> p d q", d=3)
            msk3 = wmask[:].rearrange("p (d q) -> p d q", d=3)
            for h in range(8):
                kb.cp("pool", bt, bt[:, :, h, :], msk3, reads=(wmask,))
            for h in range(8):
                for b in range(32):
                    tb = tmpb[(h * 32 + b) % 2]
                    kb.ts("dve", tb, tb[:], idx3, float(b), rbb[:, b * 8 + h:b * 8 + h + 1], ALU.is_equal, ALU.mult,
                          reads=(widx, rbb))
                    kb.tt("pool", bt, bt[:, :, h, :], bt[:, :, h, :], tb[:], ALU.add, reads=(tb, bt))
            kb.dma("sp", S["btd"], bt[:].rearrange("p d h q -> p (d h q)"), bt, False)
            cTf = kb.sb("cTf", [128, 16, 2], F32, dma=True)
            for kc in range(16):
                kb.dma("sp", cTf[:, kc, :], I["c2"][:, kc * 128:(kc + 1) * 128].rearrange("s p -> p s"), cTf, True, slow=True)
            kb.op("pool", lambda e: e.memset(cT[:], 0.0), writes=(cT,))
            kb.cp("dve", cT, cT[:, :, 0:2], cTf[:], reads=(cTf,))
            g_ = ada_gen(kb, 0, I, S, cT)
            for _ in g_:
                pass

        for l in range(L):
            if l == 0:
                phase_norm(kb, l, I, S, xsrc, ident, NCH, seq_of)
            phase_proj(kb, l, I, S, ident, rmat, (ones, onesH, rmatH, gall), NCH, seqs, seq_of)
            phase_dense_attn(kb, l, S, ident, seqs, mla=True)
            phase_dense_attn(kb, l, S, ident, seqs, mla=False)
            phase_win_attn(kb, l, I, S, ident, seqs)
            phase_branch(kb, l, I, S, NCH, side=((lambda: ada_gen(kb, l + 1, I, S, cT)) if l + 1 < L else None))
            phase_out(kb, l, I, S, xsrc, xdst, NCH, seq_of, ident, fuse=(l < L - 1))
        print("recorded ops:", kb.nops)
    return nc


def phase_norm(kb, l, I, S, xsrc, ident, NCH, seq_of):
    with kb.phase("norm%d" % l):
        xt = Rot([kb.sb("xt%d" % i, [128, D], F32, dma=True) for i in range(3)])
        junk = kb.sb("junk", [128, D], BF16)
        ss = Rot([kb.sb("ss%d" % i, [128, 1], F32) for i in range(2)])
        lnv = Rot([kb.sb("lnv%d" % i, [128, 1], F32) for i in range(2)])
        rstd = Rot([kb.sb("rstd%d" % i, [128, 1], F32) for i in range(2)])
        t1 = Rot([kb.sb("t1_%d" % i, [128, D], F32) for i in range(2)])
        hb = Rot([kb.sb("hb%d" % i, [128, D], BF16) for i in range(2)])
        hTt = Rot([kb.sb("hTt%d" % i, [128, 16, 512], BF16, dma=True) for i in range(2)])
        gsb = kb.sb("gsb", [128, D], F32, dma=True)
        shb = kb.sb("shb", [128, D], F32, dma=True)
        pT = Rot([kb.pb("pT%d" % i, [128, 512], BF16) for i in range(4)])
        cur_seq = -1
        for c in range(NCH):
            t0 = c * 512
            s = seq_of(t0)
            if s != cur_seq:
                cur_seq = s
                kb.dma("sp", shb[:], S["modd"][l, s, 0:D].partition_broadcast(128), shb, True)
                kb.dma("sp", gsb[:], S["modd"][l, s, D:2 * D].partition_broadcast(128), gsb, True)
            ht = hTt.next()
            for i in range(4):
                ti = c * 4 + i
                if ti == 0:
                    xq = [xt.next()]
                    kb.dma("sp", xq[0][:], xsrc(l, 0, 128), xq[0], True)
                x = xq.pop(0)
                if ti + 1 < NCH * 4:
                    xn_ = xt.next()
                    kb.dma("sp", xn_[:], xsrc(l, (ti + 1) * 128, 128), xn_, True)
                    xq.append(xn_)
                s_, ln_, r_, t_, h_ = ss.next(), lnv.next(), rstd.next(), t1.next(), hb.next()
                kb.actf(junk, junk[:], x[:], AF.Square, reads=(x,), accum=s_[:], extra_w=(s_,))
                kb.actf(ln_, ln_[:], s_[:], AF.Ln, reads=(s_,), scale=1.0 / D, bias=EPS)
                kb.actf(r_, r_[:], ln_[:], AF.Exp, reads=(ln_,), scale=-0.5)
                kb.stt("dve", t_, t_[:], x[:], r_[:, 0:1], gsb[:], ALU.mult, ALU.mult, reads=(x, r_, gsb))
                kb.tt("pool", h_, h_[:], t_[:], shb[:], ALU.add, reads=(t_, shb))
                for g in range(4):
                    p = pT.next()
                    for j in range(4):
                        kc = g * 4 + j
                        kb.op("pe", lambda e, p=p, j=j, h_=h_, kc=kc: e.transpose(
                            p[:, j * 128:(j + 1) * 128], h_[:, kc * 128:(kc + 1) * 128], ident[:]),
                            reads=(h_, ident), writes=(p,), sig=(j == 3))
                    eng = "act" if g % 2 == 0 else "dve"
                    kb.cp(eng, ht, ht[:, g * 4:(g + 1) * 4, i * 128:(i + 1) * 128],
                          p[:].rearrange("p (j t) -> p j t", j=4), reads=(p,))
            kb.dma("sp", S["hT"].rearrange("(kc p) t -> p kc t", p=128)[:, :, t0:t0 + 512], ht[:], ht, False)


def linear_fm(kb, name, in_dram, kcn, wgroups, NCH, seqs, seq_of, I, ident, rmat, ones, load_gains, kr_ctx=None):
    SP = kb.SP
    with kb.phase(name):
        ones, onesH, rmatH, gcol = ones
        gbase = 16 * load_gains

        def G(spec):
            return gcol[0:spec[2], gbase + spec[1]:gbase + spec[1] + 1]
        maxc = max(sum(n for _, n in g["srcs"]) for g in wgroups)
        wt_l = [kb.sb("wt%d" % i, [128, kcn, maxc + 64], BF16, dma=True) for i in range(2)]
        for v in wt_l:
            kb.op("pool", lambda e, v=v: e.memset(v[:, :, maxc:maxc + 64], 0.0), writes=(v,))
        wt = Rot(wt_l)
        xin = Rot([kb.sb("xin%d" % i, [128, kcn, 512], BF16, dma=True) for i in range(2)])
        raw = Rot([kb.sb("raw%d" % i, [128, 512], F32) for i in range(8)])
        sq_l = [kb.sb("sq%d" % i, [128, 512], BF16) for i in range(8)]
        for v in sq_l:
            kb.op("pool", lambda e, v=v: e.memset(v[:], 0.0), writes=(v,))
        sq = Rot(sq_l)
        lnv = Rot([kb.sb("lnv%d" % i, [128, 512], F32) for i in range(2)])
        rstd = Rot([kb.sb("rstd%d" % i, [128, 512], F32) for i in range(4)])
        xn = Rot([kb.sb("xn%d" % i, [128, 512], F32) for i in range(4)])
        xb_l = [kb.sb("xb%d" % i, [128, 512], BF16) for i in range(4)]
        for v in xb_l:
            kb.op("pool", lambda e, v=v: e.memset(v[:], 0.0), writes=(v,))
        xb = Rot(xb_l)
        t1 = Rot([kb.sb("t1_%d" % i, [128, 512], F32) for i in range(2)])
        t2 = Rot([kb.sb("t2_%d" % i, [128, 512], F32) for i in range(2)])
        ob = Rot([kb.sb("ob%d" % i, [128, 512], BF16, dma=True) for i in range(4)])
        ob32 = Rot([kb.sb("ob32_%d" % i, [128, 512], F32, dma=True) for i in range(2)])
        tmc = max([j["ncols"] for g in wgroups for j in g["jobs"] if j["kind"] == "tm"] + [0])
        if tmc:
            obt = Rot([kb.sb("obt%d" % i, [128, tmc], BF16, dma=True) for i in range(3)])
        need_ax = any(j.get("tabs") == "ax" for g in wgroups for j in g["jobs"])
        need_m = any(j.get("tabs") == "m" for g in wgroups for j in g["jobs"]) or kr_ctx is not None
        if need_ax:
            cosa = Rot([kb.sb("cosa%d" % i, [128, 512], F32, dma=True) for i in range(3)])
            sina = Rot([kb.sb("sina%d" % i, [128, 512], F32, dma=True) for i in range(3)])
        if need_m:
            cosm = Rot([kb.sb("cosm%d" % i, [64, 512], F32, dma=True) for i in range(3)])
            sinm = Rot([kb.sb("sinm%d" % i, [64, 512], F32, dma=True) for i in range(3)])
        if kr_ctx is not None:
            krr = Rot([kb.sb("krr%d" % i, [64, 512], F32, dma=True) for i in range(2)])
            sqkr_l = [kb.sb("sqkr%d" % i, [128, 512], BF16) for i in range(2)]
            for v in sqkr_l:
                kb.op("pool", lambda e, v=v: e.memset(v[:], 0.0), writes=(v,))
            sqkr = Rot(sqkr_l)
            krg = Rot([kb.sb("krg%d" % i, [64, 512], F32) for i in range(2)])
            krgb_l = [kb.sb("krgb%d" % i, [128, 512], BF16) for i in range(2)]
            for v in krgb_l:
                kb.op("pool", lambda e, v=v: e.memset(v[:], 0.0), writes=(v,))
            krgb = Rot(krgb_l)
            krt = Rot([kb.sb("krt%d" % i, [64, 512], F32) for i in range(2)])
            krrope = Rot([kb.sb("krrope%d" % i, [64, 512], F32) for i in range(2)])
        pm = Rot([kb.pb("pm%d" % i, [128, 512]) for i in range(4)])
        pss = Rot([kb.pb("pss%d" % i, [128, 512]) for i in range(2)])
        prot = Rot([kb.pb("prot%d" % i, [128, 512]) for i in range(2)])
        pend = deque()

        def advance():
            for _ in range(len(pend)):
                g = pend.popleft()
                try:
                    next(g)
                    pend.append(g)
                except StopIteration:
                    pass

        def store(buf, rows, dst, t0):
            kb.dma("sp", dst[0:rows, t0:t0 + 512], buf[0:rows, :], buf, False)

        def job_gen(job, x, w, t0, tabs, krs):
            kind = job["kind"]
            if kind == "act":
                p = pm.next()
                off = job["off"]
                for kc in range(kcn):
                    kb.mm(p, p[:], w[:, kc, off:off + 128], x[:, kc, :], reads=(w, x), start=(kc == 0),
                          stop=(kc == kcn - 1), sig=(kc == kcn - 1))
                o = ob.next()
                kb.actf(o, o[:], p[:], job["func"], reads=(p,))
                store(o, 128, job["out"], t0)
                return
            if kind == "tm":
                off, ncols = job["off"], job["ncols"]
                for i in range(4):
                    o = obt.next()
                    for nb in range(ncols // 512):
                        p = pm.next()
                        for kc in range(kcn):
                            kb.mm(p, p[:], x[:, kc, i * 128:(i + 1) * 128], w[:, kc, off + nb * 512:off + (nb + 1) * 512],
                                  reads=(x, w), start=(kc == 0), stop=(kc == kcn - 1), sig=(kc == kcn - 1))
                        kb.cp("act" if nb % 2 == 0 else "dve", o, o[:, nb * 512:(nb + 1) * 512], p[:], reads=(p,))
                    for (dst, doff, n) in job["outs"]:
                        kb.dma("sp", dst[t0 + i * 128:t0 + (i + 1) * 128, :], o[:, doff:doff + n], o, False)
                return
            if kind == "raw32":
                p = pm.next()
                off, rows = job["off"], job["rows"]
                for kc in range(kcn):
                    kb.mm(p, p[0:rows, :], w[:, kc, off:off + rows], x[:, kc, :], reads=(w, x), start=(kc == 0),
                          stop=(kc == kcn - 1), sig=(kc == kcn - 1))
                o = ob32.next()
                kb.cp("act", o, o[0:rows, :], p[0:rows, :], reads=(p,))
                store(o, rows, job["out"], t0)
                return
            blocks = job["blocks"]
            ps_, raws, sqs = [], [], []
            for (off, rows, gain, rope, out) in blocks:
                p = pm.next()
                for kc in range(kcn):
                    kb.mm(p, p[:], w[:, kc, off:off + 128], x[:, kc, :], reads=(w, x), start=(kc == 0),
                          stop=(kc == kcn - 1), sig=(kc == kcn - 1))
                ps_.append(p)
            for bi, (off, rows, gain, rope, out) in enumerate(blocks):
                r_, s_ = raw.next(), sq.next()
                kb.cp("act", r_, r_[0:rows, :], ps_[bi][0:rows, :], reads=(ps_[bi],))
                kb.actf(s_, s_[0:rows, :], ps_[bi][0:rows, :], AF.Square, reads=(ps_[bi],))
                raws.append(r_)
                sqs.append(s_)
            yield
            if kcn <= 4:
                yield
            pz = pss.next()
            nsq = len(blocks) + (1 if job.get("shared_kr") else 0)
            for bi, (off, rows, gain, rope, out) in enumerate(blocks):
                kb.mm(pz, pz[:], (ones if rows == 128 else onesH)[:, :], sqs[bi][:, :], reads=(ones, onesH, sqs[bi]),
                      start=(bi == 0), stop=(bi == nsq - 1), sig=(bi == nsq - 1))
            if job.get("shared_kr"):
                kb.mm(pz, pz[:], onesH[:, :], krs["sq"][:, :], reads=(onesH, krs["sq"]), start=False, stop=True)
            ln_, rs_ = lnv.next(), rstd.next()
            kb.actf(ln_, ln_[:], pz[:], AF.Ln, reads=(pz,), scale=1.0 / job["d"], bias=EPS)
            kb.actf(rs_, rs_[:], ln_[:], AF.Exp, reads=(ln_,), scale=-0.5)
            ropes = []
            for bi, (off, rows, gain, rope, out) in enumerate(blocks):
                if not rope:
                    o = ob.next()
                    kb.stt("dve", o, o[0:rows, :], raws[bi][0:rows, :], G(gain), rs_[0:rows, :], ALU.mult, ALU.mult,
                           reads=(raws[bi], rs_, gcol))
                    store(o, rows, out, t0)
                else:
                    n_, b_ = xn.next(), xb.next()
                    kb.stt("dve", n_, n_[0:rows, :], raws[bi][0:rows, :], G(gain), rs_[0:rows, :], ALU.mult, ALU.mult,
                           reads=(raws[bi], rs_, gcol))
                    kb.cp("act", b_, b_[0:rows, :], n_[0:rows, :], reads=(n_,))
                    ropes.append((rows, n_, b_, out))
            if job.get("shared_kr"):
                o = ob.next()
                kb.tt("dve", o, o[0:64, :], krs["rope"][0:64, :], rs_[0:64, :], ALU.mult, reads=(krs["rope"], rs_))
                store(o, 64, job["kr_out"], t0)
            if not ropes:
                return
            yield
            yield
            if kcn <= 4:
                yield
            cs, sn = tabs[job["tabs"]]
            for (rows, n_, b_, out) in ropes:
                pr = prot.next()
                kb.mm(pr, pr[:], (rmat if rows == 128 else rmatH)[:, :], b_[:, :], reads=(rmat, rmatH, b_))
                a_, c_ = t1.next(), t2.next()
                kb.tt("dve", a_, a_[0:rows, :], n_[0:rows, :], cs[0:rows, :], ALU.mult, reads=(n_, cs))
                kb.tt("dve", c_, c_[0:rows, :], pr[0:rows, :], sn[0:rows, :], ALU.mult, reads=(pr, sn))
                o = ob.next()
                kb.tt("pool", o, o[0:rows, :], a_[0:rows, :], c_[0:rows, :], ALU.add, reads=(a_, c_))
                store(o, rows, out, t0)

        in_v = in_dram.rearrange("(kc p) t -> p kc t", p=128)

        def load_w(g):
            w = wt.next()
            o_ = 0
            for (src, n) in g["srcs"]:
                kb.dma("pool", w[:, :, o_:o_ + n], src.rearrange("(kc p) n -> p kc n", p=128), w, True)
                o_ += n
            return w

        def prefetch(g, c):
            t0 = c * 512
            s = seq_of(t0)
            p0 = t0 - seqs[s][0]
            d = {}
            x = xin.next()
            kb.dma("sp", x[:], in_v[:, :, t0:t0 + 512], x, True)
            d["x"] = x
            tabs = {}
            if need_ax and any(j.get("tabs") == "ax" for j in g["jobs"]):
                ca, sa = cosa.next(), sina.next()
                kb.dma("sp", ca[:], I["k_cos_ax"][:, p0:p0 + 512], ca, True)
                kb.dma("sp", sa[:], I["k_sin_ax"][:, p0:p0 + 512], sa, True)
                tabs["ax"] = (ca, sa)
            if need_m and (kr_ctx is not None or any(j.get("tabs") == "m" for j in g["jobs"])):
                cm, sm_ = cosm.next(), sinm.next()
                kb.dma("sp", cm[:], I["k_cos_m"][:, p0:p0 + 512], cm, True)
                kb.dma("sp", sm_[:], I["k_sin_m"][:, p0:p0 + 512], sm_, True)
                tabs["m"] = (cm, sm_)
            d["tabs"] = tabs
            if kr_ctx is not None:
                kr_ = krr.next()
                kb.dma("sp", kr_[:], kr_ctx["src"][:, t0:t0 + 512], kr_, True)
                d["kr"] = kr_
            return d

        steps = [(gi, c) for gi in range(len(wgroups)) for c in range(NCH)]
        wts = {0: load_w(wgroups[0])}
        nxt = prefetch(wgroups[0], 0)
        for si, (gi, c) in enumerate(steps):
            g = wgroups[gi]
            if c == 0 and gi + 1 < len(wgroups):
                wts[gi + 1] = load_w(wgroups[gi + 1])
            cur = nxt
            if si + 1 < len(steps):
                nxt = prefetch(wgroups[steps[si + 1][0]], steps[si + 1][1])
            w = wts[gi]
            t0 = c * 512
            x, tabs = cur["x"], cur["tabs"]
            krs = None
            if kr_ctx is not None and any(j.get("shared_kr") for j in g["jobs"]):
                kr_ = cur["kr"]
                sk_, kg_, kgb_, kt_, kro_ = sqkr.next(), krg.next(), krgb.next(), krt.next(), krrope.next()
                kb.actf(sk_, sk_[0:64, :], kr_[:], AF.Square, reads=(kr_,))
                kb.ts("dve", kg_, kg_[:], kr_[:], G(kr_ctx["gain"]), None, ALU.mult, None, reads=(kr_, gcol))
                kb.cp("act", kgb_, kgb_[0:64, :], kg_[:], reads=(kg_,))
                pr = prot.next()
                kb.mm(pr, pr[:], rmatH[:, :], kgb_[:, :], reads=(rmatH, kgb_))
                kb.tt("dve", kt_, kt_[:], kg_[:], tabs["m"][0][:], ALU.mult, reads=(kg_, tabs["m"][0]))
                kb.tt("dve", kro_, kro_[:], pr[0:64, :], tabs["m"][1][:], ALU.mult, reads=(pr, tabs["m"][1]))
                kb.tt("pool", kro_, kro_[:], kro_[:], kt_[:], ALU.add, reads=(kro_, kt_))
                krs = {"sq": sk_, "rope": kro_}
            for job in g["jobs"]:
                gen = job_gen(job, x, w, t0, tabs, krs)
                try:
                    next(gen)
                    alive = True
                except StopIteration:
                    alive = False
                advance()
                if alive:
                    pend.append(gen)
        while pend:
            advance()


def linear_tm(kb, name, in_dram, kcn, srcs, outs, NCH):
    with kb.phase(name):
        ncols = sum(n for _, n in srcs)
        w = kb.sb("wtm", [128, kcn, ncols], BF16, dma=True)
        xin = Rot([kb.sb("xin%d" % i, [128, kcn, 512], BF16, dma=True) for i in range(2)])
        ob = Rot([kb.sb("ob%d" % i, [128, ncols], BF16, dma=True) for i in range(3)])
        pm = Rot([kb.pb("pm%d" % i, [128, 512]) for i in range(4)])
        o_ = 0
        for (src, n) in srcs:
            kb.dma("pool", w[:, :, o_:o_ + n], src.rearrange("(kc p) n -> p kc n", p=128), w, True)
            o_ += n
        in_v = in_dram.rearrange("(kc p) t -> p kc t", p=128)
        def pf(c):
            x = xin.next()
            kb.dma("sp", x[:], in_v[:, :, c * 512:c * 512 + 512], x, True)
            return x
        nx = pf(0)
        for c in range(NCH):
            t0 = c * 512
            x = nx
            if c + 1 < NCH:
                nx = pf(c + 1)
            for i in range(4):
                o = ob.next()
                for nb in range(ncols // 512):
                    p = pm.next()
                    for kc in range(kcn):
                        kb.mm(p, p[:], x[:, kc, i * 128:(i + 1) * 128], w[:, kc, nb * 512:(nb + 1) * 512], reads=(x, w),
                              start=(kc == 0), stop=(kc == kcn - 1), sig=(kc == kcn - 1))
                    kb.cp("act" if nb % 2 == 0 else "dve", o, o[:, nb * 512:(nb + 1) * 512], p[:], reads=(p,))
                for (dst, off, n) in outs:
                    kb.dma("sp", dst[t0 + i * 128:t0 + (i + 1) * 128, :], o[:, off:off + n], o, False)


def phase_proj(kb, l, I, S, ident, rmat, ones, NCH, seqs, seq_of):
    T = kb.T
    win = I["w_in"][l]

    def load_gains(gcol):
        specs = [(I["q_a_norm"][l, 0:128], 0, 128), (I["q_a_norm"][l, 128:256], 1, 128), (I["q_a_norm"][l, 256:384], 2, 128),
                 (I["q_a_norm"][l, 384:512], 3, 128), (I["kv_a_norm"][l, 0:128], 4, 128), (I["kv_a_norm"][l, 128:256], 5, 128),
                 (I["mla_qn"][l, 0:128], 6, 128), (I["mla_qn"][l, 128:192], 7, 64), (I["mla_kn"][l, 0:128], 8, 128),
                 (I["mla_kn"][l, 128:192], 9, 64), (I["ax_qn"][l, :], 10, 128), (I["ax_kn"][l, :], 11, 128),
                 (I["win_qn"][l, :], 12, 128), (I["win_kn"][l, :], 13, 128)]
        for (src, c, rows) in specs:
            kb.dma("sp", gcol[0:rows, c:c + 1], col(src), gcol, True)

    def rows_of(dst, r0, n):
        return dst[r0:r0 + n, :]

    groups = []
    jobs_lat = [
        {"kind": "norm", "d": 512, "blocks": [(128 * i, 128, ("g", i, 128), False, rows_of(S["cqnT"], 128 * i, 128)) for i in range(4)]},
        {"kind": "norm", "d": 256, "blocks": [(512 + 128 * i, 128, ("g", 4 + i, 128), False, rows_of(S["ckvnT"], 128 * i, 128)) for i in range(2)]},
        {"kind": "raw32", "off": 768, "rows": 64, "out": S["krraw"]},
    ]
    groups.append({"srcs": [(win[:, 0:832], 832)], "jobs": jobs_lat})

    def act_group(c0, nblk, func, dst, r0):
        gs = []
        b = 0
        while b < nblk:
            n = min(8, nblk - b)
            gs.append({"srcs": [(win[:, c0 + 128 * b:c0 + 128 * (b + n)], 128 * n)],
                       "jobs": [{"kind": "act", "off": 128 * i, "func": func, "out": rows_of(dst, r0 + 128 * (b + i), 128)} for i in range(n)]})
            b += n
        return gs

    def head_group(c0, nh, gc, rope, dst):
        return {"srcs": [(win[:, c0:c0 + 128 * nh], 128 * nh)],
                "jobs": [{"kind": "norm", "d": 128, "tabs": "ax",
                          "blocks": [(128 * i, 128, ("g", gc, 128), rope, rows_of(dst, 128 * i, 128))]} for i in range(nh)]}

    groups += act_group(C_GM, 8, AF.Silu, S["gT"], 0)
    groups.append(head_group(C_AQ, 8, 10, True, S["aqT"]))
    gk = {"srcs": [(win[:, C_AK:C_AK + 256], 256), (win[:, C_WK:C_WK + 256], 256)], "jobs": []}
    for i in range(2):
        gk["jobs"].append({"kind": "norm", "d": 128, "tabs": "ax", "blocks": [(128 * i, 128, ("g", 11, 128), True, rows_of(S["akT"], 128 * i, 128))]})
    for i in range(2):
        gk["jobs"].append({"kind": "norm", "d": 128, "blocks": [(256 + 128 * i, 128, ("g", 13, 128), False, rows_of(S["wkT"], 128 * i, 128))]})
    groups.append(gk)
    groups += act_group(C_GA, 8, AF.Silu, S["gT"], 1024)
    groups.append(head_group(C_WQ, 8, 12, False, S["wqT"]))
    groups += act_group(C_GW, 8, AF.Silu, S["gT"], 2048)
    groups += act_group(C_MM, 16, AF.Sigmoid, S["smT"], 0)
    groups += act_group(C_MA, 16, AF.Sigmoid, S["smT"], 2048)
    groups += act_group(C_MW, 16, AF.Sigmoid, S["smT"], 4096)

    groups.insert(0, {"srcs": [(win[:, C_AV:C_AV + 256], 256), (win[:, C_WV:C_WV + 256], 256)],
                      "jobs": [{"kind": "tm", "off": 0, "ncols": 512, "outs": [(S["vaw"], 0, 512)]}]})
    linear_fm(kb, "projA%d" % l, S["hT"], 16, groups, NCH, seqs, seq_of, I, ident, rmat, ones, l, None)
    wq = I["w_q_up"][l]
    jobs_q = [{"kind": "norm", "d": 192, "tabs": "m",
               "blocks": [(192 * h, 128, ("g", 6, 128), False, rows_of(S["qnT"], 128 * h, 128)),
                          (192 * h + 128, 64, ("g", 7, 64), True, rows_of(S["qrT"], 64 * h, 64))]} for h in range(8)]
    linear_fm(kb, "projQ%d" % l, S["cqnT"], 4, [{"srcs": [(wq, 1536)], "jobs": jobs_q}], NCH, seqs, seq_of, I, ident, rmat,
            ones, l, None)
    wkv = I["w_kv_up"][l]
    srcs_k = [(wkv[:, 256 * h:256 * h + 128], 128) for h in range(8)]
    jobs_k = [{"kind": "norm", "d": 192, "shared_kr": True, "kr_out": rows_of(S["krT"], 64 * h, 64),
               "blocks": [(128 * h, 128, ("g", 8, 128), False, rows_of(S["knT"], 128 * h, 128))]} for h in range(8)]
    srcs_v = [(wkv[:, 256 * h + 128:256 * h + 256], 128) for h in range(8)]
    gv = {"srcs": srcs_v, "jobs": [{"kind": "tm", "off": 0, "ncols": 1024, "outs": [(S["vm"], 0, 1024)]}]}
    linear_fm(kb, "projK%d" % l, S["ckvnT"], 2, [{"srcs": srcs_k, "jobs": jobs_k}, gv], NCH, seqs, seq_of, I, ident, rmat,
              ones, l, {"src": S["krraw"], "gain": ("g", 9, 64)})


def phase_dense_attn(kb, l, S, ident, seqs, mla, side=None):
    name = ("mla%d" if mla else "ax%d") % l
    scale = (192.0 if mla else 128.0) ** -0.5
    br = 0 if mla else 1
    with kb.phase(name):
        Smax = max(s for _, s in seqs)
        nb = 2
        qn = Rot([kb.sb("qn%d" % i, [128, Smax], BF16, dma=True) for i in range(nb)])
        kn = Rot([kb.sb("kn%d" % i, [128, Smax], BF16, dma=True) for i in range(nb)])
        if mla:
            qr_l = [kb.sb("qr%d" % i, [128, Smax], BF16, dma=True) for i in range(nb)]
            kr_l = [kb.sb("kr%d" % i, [128, Smax], BF16, dma=True) for i in range(nb)]
            for v in qr_l + kr_l:
                kb.op("pool", lambda e, v=v: e.memset(v[64:128, :], 0.0), writes=(v,))
            qr, kr = Rot(qr_l), Rot(kr_l)
        va_l = [kb.sb("va%d" % i, [128, Smax // 128, 129], BF16, dma=True) for i in range(nb)]
        for v in va_l:
            kb.op("pool", lambda e, v=v: e.memset(v[:, :, 128:129], 1.0), writes=(v,))
        va = Rot(va_l)
        gt = Rot([kb.sb("gt%d" % i, [128, Smax], BF16, dma=True) for i in range(nb)])
        pt = Rot([kb.sb("pt%d" % i, [128, 512], BF16) for i in range(4)])
        rden = Rot([kb.sb("rden%d" % i, [128, 1], F32) for i in range(8)])
        on = Rot([kb.sb("on%d" % i, [128, 128], BF16) for i in range(8)])
        at = Rot([kb.sb("at%d" % i, [128, 512], BF16, dma=True) for i in range(2)])
        ps = Rot([kb.pb("ps%d" % i, [128, 512]) for i in range(3)])
        accs = Rot([[kb.pb("acc%d_%d" % (s_, i), [128, 512]) for i in range(2)] for s_ in range(2)])
        pT = kb.pb("pT", [128, 512], BF16)

        work = []
        for (tok0, Sq) in seqs:
            for h in range(8):
                work.append((tok0, Sq, h))
        loaded = {}

        def load(wi):
            tok0, Sq, h = work[wi]
            kvh = h if mla else h // 4
            d = {"qn": qn.next(), "kn": kn.next(), "va": va.next(), "gt": gt.next()}
            qsrc, ksrc = (S["qnT"], S["knT"]) if mla else (S["aqT"], S["akT"])
            kb.dma("sp", d["qn"][:, 0:Sq], qsrc[128 * h:128 * h + 128, tok0:tok0 + Sq], d["qn"], True)
            kb.dma("sp", d["kn"][:, 0:Sq], ksrc[128 * kvh:128 * kvh + 128, tok0:tok0 + Sq], d["kn"], True)
            if mla:
                d["qr"], d["kr"] = qr.next(), kr.next()
                kb.dma("sp", d["qr"][0:64, 0:Sq], S["qrT"][64 * h:64 * h + 64, tok0:tok0 + Sq], d["qr"], True)
                kb.dma("sp", d["kr"][0:64, 0:Sq], S["krT"][64 * h:64 * h + 64, tok0:tok0 + Sq], d["kr"], True)
                vsrc = S["vm"][tok0:tok0 + Sq, 128 * h:128 * h + 128]
            else:
                vsrc = S["vaw"][tok0:tok0 + Sq, 128 * kvh:128 * kvh + 128]
            kb.dma("sp", d["va"][:, 0:Sq // 128, 0:128], vsrc.rearrange("(c p) d -> p c d", p=128), d["va"], True)
            kb.dma("sp", d["gt"][:, 0:Sq], S["gT"][br * 1024 + 128 * h:br * 1024 + 128 * h + 128, tok0:tok0 + Sq], d["gt"], True)
            loaded[wi] = d

        deferred = []

        def run_deferred():
            for f in deferred:
                f()
            del deferred[:]

        LOOK = 2
        load(0)
        for wi, (tok0, Sq, h) in enumerate(work):
            run_deferred()
            if wi + 1 < len(work):
                load(wi + 1)
            d = loaded.pop(wi)
            nk = Sq // 128
            if side is not None:
                try:
                    next(side)
                except StopIteration:
                    side = None

            def qk(qc, kc, d=d):
                p = ps.next()
                ks = slice(kc * 128, kc * 128 + 128)
                qs = slice(qc * 512, qc * 512 + 512)
                kb.mm(p, p[:], d["kn"][:, ks], d["qn"][:, qs], reads=(d["kn"], d["qn"]), start=True, stop=not mla,
                      sig=not mla)
                if mla:
                    kb.mm(p, p[:], d["kr"][:, ks], d["qr"][:, qs], reads=(d["kr"], d["qr"]), start=False, stop=True)
                return p

            def pv(kc, p, acc, d=d, nk=nk):
                t = pt.next()
                kb.actf(t, t[:], p[:], AF.Exp, reads=(p,), scale=scale)
                for j in range(4):
                    ab = acc[j // 2]
                    o0 = 256 * (j % 2)
                    kb.mm(ab, ab[:, o0:o0 + 129], t[:, j * 128:(j + 1) * 128], d["va"][:, kc, :], reads=(t, d["va"]),
                          start=(kc == 0 and j % 2 == 0), stop=(kc == nk - 1), sig=(kc == nk - 1))

            its = [(qc, kc) for qc in range(Sq // 512) for kc in range(nk)]
            pq = deque()
            for idx in range(min(LOOK, len(its))):
                pq.append(qk(*its[idx]))
            acc = None
            for idx, (qc, kc) in enumerate(its):
                if idx + LOOK < len(its):
                    pq.append(qk(*its[idx + LOOK]))
                if kc == 0:
                    acc = accs.next()
                pv(kc, pq.popleft(), acc)
                if kc == 1:
                    run_deferred()
                if kc == nk - 1:
                    ons = []
                    for j in range(4):
                        ab = acc[j // 2]
                        o0 = 256 * (j % 2)
                        r_, o_ = rden.next(), on.next()
                        kb.op("dve", lambda e, r_=r_, ab=ab, o0=o0: e.reciprocal(out=r_[:], in_=ab[:, o0 + 128:o0 + 129]),
                              reads=(ab,), writes=(r_,))
                        kb.ts("dve", o_, o_[:], ab[:, o0:o0 + 128], r_[:, 0:1], None, ALU.mult, None, reads=(ab, r_))
                        ons.append(o_)

                    def fin(ons=ons, qc=qc, d=d, h=h, tok0=tok0):
                        for j in range(4):
                            kb.op("pe", lambda e, j=j, o_=ons[j]: e.transpose(pT[:, j * 128:(j + 1) * 128], o_[:], ident[:]),
                                  reads=(ons[j], ident), writes=(pT,), sig=(j == 3))
                        a = at.next()
                        kb.tt("dve", a, a[:], pT[:], d["gt"][:, qc * 512:qc * 512 + 512], ALU.mult, reads=(pT, d["gt"]))
                        kb.dma("sp", S["AT"][br * 1024 + 128 * h:br * 1024 + 128 * h + 128,
                                             tok0 + qc * 512:tok0 + qc * 512 + 512], a[:], a, False)
                    deferred.append(fin)
        run_deferred()
        if side is not None:
            for _ in side:
                pass


def ada_gen(kb, l, I, S, cT):
    wada = Rot([kb.sb("wada%d" % i, [128, 16, 512], BF16, dma=True) for i in range(2)])
    bb = Rot([kb.sb("adab%d" % i, [2, 512], F32, dma=True) for i in range(2)])
    lg = Rot([kb.sb("adag%d" % i, [2, 512], F32, dma=True) for i in range(2)])
    mr = Rot([kb.sb("adam%d" % i, [2, 512], F32, dma=True) for i in range(2)])
    pmod = kb.pb("pmod", [128, 512])

    def pf(j):
        w, b_ = wada.next(), bb.next()
        kb.dma("pool", w[:], I["w_ada"][l, :, j * 512:(j + 1) * 512].rearrange("(kc p) n -> p kc n", p=128), w, True)
        kb.dma("sp", b_[:], I["b_ada"][l, j * 512:(j + 1) * 512].partition_broadcast(2), b_, True)
        g_ = None
        if 4 <= j < 8:
            g_ = lg.next()
            kb.dma("sp", g_[:], I["ln_g"][l, (j - 4) * 512:(j - 3) * 512].partition_broadcast(2), g_, True)
        return w, b_, g_
    nx = pf(0)
    for j in range(12):
        w, b_, g_ = nx
        if j + 1 < 12:
            nx = pf(j + 1)
        for kc in range(16):
            kb.mm(pmod, pmod[:], cT[:, kc, :], w[:, kc, :], reads=(cT, w), start=(kc == 0), stop=(kc == 15), sig=(kc == 15))
        m_ = mr.next()
        kb.tt("dve", m_, m_[:], pmod[0:2, :], b_[:], ALU.add, reads=(pmod, b_))
        if g_ is not None:
            kb.stt("dve", m_, m_[:], m_[:], 1.0, g_[:], ALU.add, ALU.mult, reads=(m_, g_))
        kb.dma("sp", S["modd"][l, :, j * 512:(j + 1) * 512], m_[:], m_, False)
        yield


def phase_win_attn(kb, l, I, S, ident, seqs):
    scale = 128.0 ** -0.5
    with kb.phase("win%d" % l):
        Smax = max(s for _, s in seqs)
        NB2 = 2
        qw = Rot([kb.sb("qw%d" % i, [128, 4, Smax], BF16, dma=True) for i in range(NB2)])
        gw = Rot([kb.sb("gw%d" % i, [128, 4, Smax], BF16, dma=True) for i in range(1)])
        kw = Rot([kb.sb("kw%d" % i, [128, Smax], BF16, dma=True) for i in range(NB2)])
        va_l = [kb.sb("va%d" % i, [128, Smax // 128, 129], BF16, dma=True) for i in range(NB2)]
        for v in va_l:
            kb.op("pool", lambda e, v=v: e.memset(v[:, :, 128:129], 1.0), writes=(v,))
        va = Rot(va_l)
        bt = kb.sb("bt", [128, 3, 8, 128], F32, dma=True)
        kb.dma("sp", bt[:].rearrange("p d h q -> p (d h q)"), S["btd"], bt, True)
        sk = kb.sb("sk", [128, 8], F32, dma=True)
        esk = kb.sb("esk", [128, 8], F32)
        kb.dma("sp", sk[:], I["win_sink"][l, :].partition_broadcast(128), sk, True)
        kb.actf(esk, esk[:], sk[:], AF.Exp, reads=(sk,))
        tt_ = Rot([kb.sb("tt%d" % i, [128, 512], F32) for i in range(4)])
        pt = Rot([kb.sb("pt%d" % i, [128, 512], BF16) for i in range(6)])
        den = Rot([kb.sb("den%d" % i, [128, 1], F32) for i in range(8)])
        rden = Rot([kb.sb("rden%d" % i, [128, 1], F32) for i in range(8)])
        on = Rot([kb.sb("on%d" % i, [128, 128], BF16) for i in range(8)])
        at = Rot([kb.sb("at%d" % i, [128, 4, 128], BF16, dma=True) for i in range(3)])
        ps = Rot([kb.pb("ps%d" % i, [128, 512]) for i in range(3)])
        accs = Rot([[kb.pb("acc%d_%d" % (s_, i), [128, 512]) for i in range(2)] for s_ in range(2)])
        pT = kb.pb("pT", [128, 512], BF16)

        units = []
        for (tok0, Sq) in seqs:
            for j in range(2):
                units.append((tok0, Sq, j))

        def load(u):
            tok0, Sq, j = u
            d = {"qw": qw.next(), "kw": kw.next(), "va": va.next()}
            kb.dma("sp", d["qw"][:, :, 0:Sq], S["wqT"][512 * j:512 * j + 512, tok0:tok0 + Sq].rearrange("(g p) t -> p g t", p=128), d["qw"], True)
            kb.dma("sp", d["kw"][:, 0:Sq], S["wkT"][128 * j:128 * j + 128, tok0:tok0 + Sq], d["kw"], True)
            kb.dma("sp", d["va"][:, 0:Sq // 128, 0:128], S["vaw"][tok0:tok0 + Sq, 256 + 128 * j:256 + 128 * j + 128].rearrange("(c p) d -> p c d", p=128), d["va"], True)
            return d

        def qblock(d, tok0, Sq, j, n):
            nblk = Sq // 128
            cs = [c for c in (n - 1, n, n + 1) if 0 <= c < nblk]
            es = []
            for c in cs:
                dd = c - n + 1
                p = ps.next()
                kb.mm(p, p[:].rearrange("p (g q) -> p g q", g=4), d["kw"][:, c * 128:(c + 1) * 128],
                      d["qw"][:, :, n * 128:(n + 1) * 128], reads=(d["kw"], d["qw"]))
                t_ = tt_.next()
                kb.stt("dve", t_, t_[:].rearrange("p (g q) -> p g q", g=4), p[:].rearrange("p (g q) -> p g q", g=4),
                       scale, bt[:, dd, 4 * j:4 * j + 4, :], ALU.mult, ALU.add, reads=(p, bt))
                e_ = pt.next()
                kb.actf(e_, e_[:], t_[:], AF.Exp, reads=(t_,))
                es.append((c, e_))
            yield
            acc = accs.next()
            for ci, (c, e_) in enumerate(es):
                for g in range(4):
                    ab = acc[g // 2]
                    o0 = 256 * (g % 2)
                    kb.mm(ab, ab[:, o0:o0 + 129], e_[:, g * 128:(g + 1) * 128], d["va"][:, c, :], reads=(e_, d["va"]),
                          start=(ci == 0 and g % 2 == 0), stop=(ci == len(es) - 1), sig=(ci == len(es) - 1))
            ons = []
            for g in range(4):
                ab = acc[g // 2]
                o0 = 256 * (g % 2)
                dn, r_, o_ = den.next(), rden.next(), on.next()
                h = 4 * j + g
                kb.tt("dve", dn, dn[:], ab[:, o0 + 128:o0 + 129], esk[:, h:h + 1], ALU.add, reads=(ab, esk))
                kb.op("dve", lambda e, r_=r_, dn=dn: e.reciprocal(out=r_[:], in_=dn[:]), reads=(dn,), writes=(r_,))
                kb.ts("dve", o_, o_[:], ab[:, o0:o0 + 128], r_[:, 0:1], None, ALU.mult, None, reads=(ab, r_))
                ons.append(o_)
            yield
            for g in range(4):
                kb.op("pe", lambda e, g=g, o_=ons[g]: e.transpose(pT[:, g * 128:(g + 1) * 128], o_[:], ident[:]),
                      reads=(ons[g], ident), writes=(pT,), sig=(g == 3))
            a = at.next()
            kb.tt("dve", a, a[:], pT[:].rearrange("p (g q) -> p g q", g=4), d["gw"][:, :, n * 128:(n + 1) * 128], ALU.mult,
                  reads=(pT, d["gw"]))
            kb.dma("sp", S["AT"][2048 + 512 * j:2048 + 512 * j + 512, tok0 + n * 128:tok0 + (n + 1) * 128].rearrange("(g p) t -> p g t", p=128),
                   a[:], a, False)

        pend = deque()

        def advance():
            for _ in range(len(pend)):
                g_ = pend.popleft()
                try:
                    next(g_)
                    pend.append(g_)
                except StopIteration:
                    pass

        nxt = load(units[0])
        for ui, (tok0, Sq, j) in enumerate(units):
            while pend:
                advance()
            d = nxt
            d["gw"] = gw.next()
            kb.dma("sp", d["gw"][:, :, 0:Sq], S["gT"][2048 + 512 * j:2048 + 512 * j + 512, tok0:tok0 + Sq].rearrange("(g p) t -> p g t", p=128), d["gw"], True)
            if ui + 1 < len(units):
                nxt = load(units[ui + 1])
            for n in range(Sq // 128):
                g_ = qblock(d, tok0, Sq, j, n)
                next(g_)
                advance()
                pend.append(g_)
        while pend:
            advance()


def phase_branch(kb, l, I, S, NCH, side=None):
    wbr = [I["w_br_mla"][l], I["w_br_ax"][l], I["w_br_win"][l]]
    with kb.phase("branch%d" % l):
        wt = Rot([[kb.sb("wb%d_%d" % (i, b), [128, 8, 512], BF16, dma=True) for b in range(3)] for i in range(2)])
        ain = Rot([[kb.sb("ain%d_%d" % (i, b), [128, 8, 512], BF16, dma=True) for b in range(3)] for i in range(2)])
        smi = Rot([[kb.sb("smi%d_%d" % (i, b), [128, 4, 512], BF16, dma=True) for b in range(3)] for i in range(2)])
        tq = Rot([[kb.sb("tq%d_%d" % (i, b), [128, 512], F32) for b in range(3)] for i in range(2)])
        u = Rot([kb.sb("u%d" % i, [128, 512], F32) for i in range(2)])
        yb = Rot([kb.sb("yb%d" % i, [128, 512], BF16, dma=True) for i in range(3)])
        pm = Rot([[kb.pb("pm%d_%d" % (i, b), [128, 512]) for b in range(3)] for i in range(2)])
        def load_w(ng):
            w = wt.next()
            for b in range(3):
                kb.dma("pool", w[b][:], wbr[b][:, ng * 512:(ng + 1) * 512].rearrange("(kc p) n -> p kc n", p=128), w[b], True)
            return w

        def pf(ng, c):
            t0 = c * 512
            a, sm = ain.next(), smi.next()
            for b in range(3):
                kb.dma("sp", a[b][:], S["AT"][1024 * b:1024 * (b + 1), t0:t0 + 512].rearrange("(kc p) t -> p kc t", p=128), a[b], True)
                kb.dma("sp", sm[b][:], S["smT"][2048 * b + 512 * ng:2048 * b + 512 * (ng + 1), t0:t0 + 512].rearrange("(k p) t -> p k t", p=128), sm[b], True)
            return a, sm

        steps = [(ng, c) for ng in range(4) for c in range(NCH)]
        wts = {0: load_w(0)}
        nx = pf(0, 0)
        sgen = side() if side is not None else None
        for si, (ng, c) in enumerate(steps):
            if c == 0 and ng + 1 < 4:
                wts[ng + 1] = load_w(ng + 1)
            if sgen is not None and si % 3 == 1:
                try:
                    next(sgen)
                except StopIteration:
                    sgen = None
            a, sm = nx
            if si + 1 < len(steps):
                nx = pf(*steps[si + 1])
            w = wts[ng]
            t0 = c * 512
            for blk in range(4):
                p, t_ = pm.next(), tq.next()
                for b in range(3):
                    for kc in range(8):
                        kb.mm(p[b], p[b][:], w[b][:, kc, blk * 128:(blk + 1) * 128], a[b][:, kc, :], reads=(w[b], a[b]),
                              start=(kc == 0), stop=(kc == 7), sig=(kc == 7))
                for b in range(3):
                    kb.tt("dve", t_[b], t_[b][:], p[b][:], sm[b][:, blk, :], ALU.mult, reads=(p[b], sm[b]))
                u_, y_ = u.next(), yb.next()
                kb.tt("pool", u_, u_[:], t_[0][:], t_[1][:], ALU.add, reads=(t_[0], t_[1]))
                kb.tt("pool", y_, y_[:], u_[:], t_[2][:], ALU.add, reads=(u_, t_[2]))
                r0 = (ng * 4 + blk) * 128
                kb.dma("sp", S["yT"][r0:r0 + 128, t0:t0 + 512], y_[:], y_, False)
        if sgen is not None:
            for _ in sgen:
                pass


def phase_out(kb, l, I, S, xsrc, xdst, NCH, seq_of, ident, fuse):
    with kb.phase("out%d" % l):
        wq_ = [kb.sb("wout%d" % q, [128, 16, 512], BF16, dma=True) for q in range(4)]
        for q in range(4):
            kb.dma("pool", wq_[q][:], I["w_out"][l][:, q * 512:(q + 1) * 512].rearrange("(kc p) n -> p kc n", p=128), wq_[q], True)
        gtb = kb.sb("gtb", [128, D], F32, dma=True)
        yin = Rot([kb.sb("yin%d" % i, [128, 16, 512], BF16, dma=True) for i in range(2)])
        xt = Rot([kb.sb("xt%d" % i, [128, D], F32, dma=True) for i in range(2)])
        xo = Rot([kb.sb("xo%d" % i, [128, D], F32, dma=True) for i in range(2)])
        tg = Rot([kb.sb("tg%d" % i, [128, 512], F32) for i in range(3)])
        pm = Rot([kb.pb("pm%d" % i, [128, 512]) for i in range(4)])
        if fuse:
            gsb = kb.sb("gsb", [128, D], F32, dma=True)
            shb = kb.sb("shb", [128, D], F32, dma=True)
            junk = kb.sb("junk", [128, D], BF16)
            ss = Rot([kb.sb("ss%d" % i, [128, 1], F32) for i in range(2)])
            lnv = Rot([kb.sb("lnv%d" % i, [128, 1], F32) for i in range(2)])
            rstd = Rot([kb.sb("rstd%d" % i, [128, 1], F32) for i in range(2)])
            t1 = kb.sb("t1", [128, D], F32)
            hb = Rot([kb.sb("hb%d" % i, [128, D], BF16) for i in range(2)])
            hTt = Rot([kb.sb("hTt%d" % i, [128, 16, 128], BF16, dma=True) for i in range(2)])
            pT = Rot([kb.pb("pT%d" % i, [128, 512], BF16) for i in range(4)])

        def pfy(c):
            y = yin.next()
            kb.dma("sp", y[:], S["yT"].rearrange("(kc p) t -> p kc t", p=128)[:, :, c * 512:c * 512 + 512], y, True)
            return y

        def pfx(ti):
            x = xt.next()
            kb.dma("sp", x[:], xsrc(l, ti * 128, 128), x, True)
            return x
        ny = pfy(0)
        nxx = pfx(0)
        cur = -1
        deferred = []
        for c in range(NCH):
            t0 = c * 512
            sq_ = seq_of(t0)
            if sq_ != cur:
                cur = sq_
                kb.dma("sp", gtb[:], S["modd"][l, sq_, 2 * D:3 * D].partition_broadcast(128), gtb, True)
                if fuse:
                    kb.dma("sp", shb[:], S["modd"][l + 1, sq_, 0:D].partition_broadcast(128), shb, True)
                    kb.dma("sp", gsb[:], S["modd"][l + 1, sq_, D:2 * D].partition_broadcast(128), gsb, True)
            y = ny
            if c + 1 < NCH:
                ny = pfy(c + 1)
            for i in range(4):
                ti = c * 4 + i
                x, o = nxx, xo.next()
                if ti + 1 < NCH * 4:
                    nxx = pfx(ti + 1)
                for nb in range(4):
                    p = pm.next()
                    for kc in range(16):
                        kb.mm(p, p[:], y[:, kc, i * 128:(i + 1) * 128], wq_[nb][:, kc, :], reads=(y, wq_[nb]),
                              start=(kc == 0), stop=(kc == 15), sig=(kc == 15))
                    t_ = tg.next()
                    ns = slice(nb * 512, (nb + 1) * 512)
                    kb.tt("dve", t_, t_[:], p[:], gtb[:, ns], ALU.mult, reads=(p, gtb))
                    kb.tt("pool", o, o[:, ns], t_[:], x[:, ns], ALU.add, reads=(t_, x))
                for f in deferred:
                    f()
                del deferred[:]
                kb.dma("sp", xdst(l, t0 + i * 128, 128), o[:], o, False)
                if fuse:
                    s_, ln_, r_, h_ = ss.next(), lnv.next(), rstd.next(), hb.next()
                    kb.actf(junk, junk[:], o[:], AF.Square, reads=(o,), accum=s_[:], extra_w=(s_,))
                    kb.actf(ln_, ln_[:], s_[:], AF.Ln, reads=(s_,), scale=1.0 / D, bias=EPS)
                    kb.actf(r_, r_[:], ln_[:], AF.Exp, reads=(ln_,), scale=-0.5)
                    kb.stt("dve", t1, t1[:], o[:], r_[:, 0:1], gsb[:], ALU.mult, ALU.mult, reads=(o, r_, gsb))
                    kb.tt("pool", h_, h_[:], t1[:], shb[:], ALU.add, reads=(t1, shb))

                    def fin(h_=h_, tt0=t0 + i * 128):
                        ht = hTt.next()
                        for g in range(4):
                            p = pT.next()
                            for j in range(4):
                                kc = g * 4 + j
                                kb.op("pe", lambda e, p=p, j=j, h_=h_, kc=kc: e.transpose(
                                    p[:, j * 128:(j + 1) * 128], h_[:, kc * 128:(kc + 1) * 128], ident[:]),
                                    reads=(h_, ident), writes=(p,), sig=(j == 3))
                            kb.cp("act" if g % 2 == 0 else "dve", ht, ht[:, g * 4:(g + 1) * 4, :],
                                  p[:].rearrange("p (j t) -> p j t", j=4), reads=(p,))
                        kb.dma("sp", S["hT"].rearrange("(kc p) t -> p kc t", p=128)[:, :, tt0:tt0 + 128], ht[:], ht, False)
                    deferred.append(fin)
        for f in deferred:
            f()


def _consts(SP):
    c = {}
    c["k_ident"] = np.eye(128, dtype=np.float32)
    R = np.zeros((128, 128), np.float32)
    for m in range(128):
        if m % 64 < 32:
            R[m + 32, m] = -1.0
        else:
            R[m - 32, m] = 1.0
    c["k_rmat"] = R
    half = 32
    freqs = (np.float32(10000.0) ** (-(np.arange(half, dtype=np.float32) / np.float32(half)))).astype(np.float32)
    pos = np.arange(SP, dtype=np.float32)
    row = np.floor(pos / 64).astype(np.float32)
    colp = (pos - row * 64).astype(np.float32)
    f128 = freqs[np.arange(128) % 32]
    posax = np.where((np.arange(128) < 64)[:, None], row[None, :], colp[None, :]).astype(np.float32)
    ang_ax = (posax * f128[:, None]).astype(np.float32)
    c["k_cos_ax"] = np.cos(ang_ax.astype(np.float64)).astype(np.float32)
    c["k_sin_ax"] = np.sin(ang_ax.astype(np.float64)).astype(np.float32)
    ang_m = (pos[None, :] * f128[:64, None]).astype(np.float32)
    c["k_cos_m"] = np.cos(ang_m.astype(np.float64)).astype(np.float32)
    c["k_sin_m"] = np.sin(ang_m.astype(np.float64)).astype(np.float32)
    k = np.arange(128)[:, None, None]
    d = np.arange(3)[None, :, None]
    q = np.arange(128)[None, None, :]
    rel = (d - 1) * 128 + k - q
    n = np.abs(rel)
    nf = np.maximum(n, 1).astype(np.float32)
    large = 8 + (np.log(nf / np.float32(8.0)).astype(np.float32) / np.float32(np.log(16.0)) * np.float32(8.0)).astype(np.float32).astype(np.int32)
    large = np.minimum(large, 15)
    bucket = np.where(rel > 0, 16, 0) + np.where(n < 8, n, large)
    valid = n <= 128
    c["k_widx"] = np.where(valid, bucket, -1).astype(np.float32).reshape(128, 384)
    c["k_wmask"] = np.where(valid, 0.0, NEGM).astype(np.float32).reshape(128, 384)
    return c


_WNAMES = ["ln_g", "w_ada", "b_ada", "w_in", "q_a_norm", "w_q_up", "kv_a_norm", "w_kv_up", "mla_qn", "mla_kn", "ax_qn",
           "ax_kn", "win_qn", "win_kn", "win_sink", "rel_bias", "w_br_mla", "w_br_ax", "w_br_win", "w_out"]


def run(inputs, n_cores, L, SP, SS, taps=()):
    nc = build_program(L, SP, SS, taps)
    consts = _consts(SP)
    shared = {k: np.ascontiguousarray(inputs[k], dtype=np.float32) for k in _WNAMES}
    shared.update(consts)
    in_maps = []
    for b in range(n_cores):
        m = dict(shared)
        m["x_p"] = np.ascontiguousarray(inputs["x_prompt"][b])
        m["x_s"] = np.ascontiguousarray(inputs["x_sample"][b])
        m["c2"] = np.ascontiguousarray(np.stack([inputs["c_prompt"][b], inputs["c_sample"][b]]))
        in_maps.append(m)
    res = run_bass_kernel_spmd(nc, in_maps, core_ids=list(range(n_cores)))
    return res


def kernel(**inputs):
    inputs = {k: np.asarray(v) for k, v in inputs.items()}
    res = run(inputs, 8, 4, 4096, 2048)
    y_p = np.stack([r["y_p"] for r in res.results]).astype(np.float32)
    y_s = np.stack([r["y_s"] for r in res.results]).astype(np.float32)
    return (y_p, y_s)
```

```python
import numpy as np
from contextlib import ExitStack
from collections import deque
import concourse.bass as bass
import concourse.mybir as mybir
from concourse.bass_utils import run_bass_kernel_spmd

F32 = mybir.dt.float32
BF16 = mybir.dt.bfloat16
AF = mybir.ActivationFunctionType
ALU = mybir.AluOpType
D = 2048
DIN = 13120
EPS = 1e-6
ENGS = ("pe", "act", "dve", "pool", "sp")
SAME_SYNC = True
NDS = 56
NEGM = -30000.0

C_CQ, C_CKV, C_KR, C_GM = 0, 512, 768, 832
C_AQ, C_AK, C_AV, C_GA = 1856, 2880, 3136, 3392
C_WQ, C_WK, C_WV, C_GW = 4416, 5440, 5696, 5952
C_MM, C_MA, C_MW = 6976, 9024, 11072


class Ev:
    __slots__ = ("key", "val")

    def __init__(self, key, val):
        self.key = key
        self.val = val


class Buf:
    def __init__(self, t, name, dsem=None):
        self.t = t
        self.name = name
        self.w = None
        self.r = []
        self.dsem = dsem

    def __getitem__(self, k):
        return self.t[k]


class KB:
    def __init__(self, nc, es, L, SP, SS):
        self.nc = nc
        self.L, self.SP, self.SS = L, SP, SS
        self.T = SP + SS
        self.es = es
        self.esem = {e: es.enter_context(nc.semaphore("s_" + e)) for e in ENGS}
        self.ecnt = {e: 0 for e in ENGS}
        self.dpool = [es.enter_context(nc.semaphore("d%d" % i)) for i in range(NDS)]
        self.dcnt = [0] * NDS
        self.dfree = list(range(NDS))
        self.q = {e: [] for e in ENGS}
        self.waited = {e: {} for e in ENGS}
        self.pending = {e: [] for e in ENGS}
        self.phase_dma = {e: {} for e in ENGS}
        self.pstack = None
        self.phase_sems = []
        self.nops = 0

    def sb(self, name, shape, dt, dma=False, persist=False):
        st = self.es if persist else self.pstack
        self.uid = getattr(self, "uid", 0) + 1
        name = "%s_%d" % (name, self.uid)
        t = st.enter_context(self.nc.sbuf_tensor(name, list(shape), dt))
        ds = None
        if dma:
            ds = self.dfree.pop()
            if not persist:
                self.phase_sems.append(ds)
        return Buf(t, name, ds)

    def pb(self, name, shape, dt=F32):
        self.uid = getattr(self, "uid", 0) + 1
        name = "%s_%d" % (name, self.uid)
        t = self.pstack.enter_context(self.nc.psum_tensor(name, list(shape), dt))
        return Buf(t, name)

    def semof(self, key):
        if key[0] == "e":
            return self.esem[key[2:]]
        return self.dpool[int(key[2:])]

    def _waits(self, eng, reads, writes):
        waits = {}

        def need(evt):
            if evt is None:
                return
            ev, e = evt
            if e == eng and (eng == "pe" or not SAME_SYNC):
                return
            assert ev.val is not None, "wait on unsignalled op"
            if self.waited[eng].get(ev.key, 0) >= ev.val:
                return
            waits[ev.key] = max(waits.get(ev.key, 0), ev.val)

        for b in reads:
            need(b.w)
        for b in writes:
            need(b.w)
            for r in b.r:
                need(r)
        for k, v in waits.items():
            self.waited[eng][k] = v
        return [(self.semof(k), v) for k, v in waits.items()]

    def op(self, eng, fn, reads=(), writes=(), sig=True):
        wl = self._waits(eng, reads, writes)
        if sig:
            self.ecnt[eng] += 1
            ev = Ev("e:" + eng, self.ecnt[eng])
            for p in self.pending[eng]:
                p.val = ev.val
            self.pending[eng] = []
        else:
            ev = Ev("e:" + eng, None)
            self.pending[eng].append(ev)
        semh = self.esem[eng]

        def run(e):
            for s, v in wl:
                e.wait_ge(s, v)
            ins = fn(e)
            if sig:
                ins.then_inc(semh, 1)

        self.q[eng].append(run)
        self.nops += 1
        for b in reads:
            b.r.append((ev, eng))
        for b in writes:
            b.w = (ev, eng)
            b.r = []
        return ev

    def dma(self, qeng, out_ap, in_ap, buf, load, slow=False):
        wl = self._waits(qeng, () if load else (buf,), (buf,) if load else ())
        di = buf.dsem
        self.dcnt[di] += 16
        ev = Ev("d:%d" % di, self.dcnt[di])
        semh = self.dpool[di]

        def run(e):
            for s, v in wl:
                e.wait_ge(s, v)
            e.dma_start(out=out_ap, in_=in_ap, allow_slow_non_contiguous=slow).then_inc(semh, 16)

        self.q[qeng].append(run)
        self.nops += 1
        if load:
            buf.w = (ev, "dma")
            buf.r = []
        else:
            buf.r.append((ev, "dma"))
        self.phase_dma[qeng][ev.key] = ev.val
        return ev

    def phase(self, name):
        kb = self

        class _P:
            def __enter__(s):
                kb.pstack = ExitStack()
                kb.pstack.__enter__()
                kb.phase_sems = []
                return s

            def __exit__(s, *a):
                if a[0] is None:
                    kb.flush(name)
                kb.pstack.__exit__(*a)
                kb.dfree.extend(kb.phase_sems)
                kb.pstack = None
                return False

        return _P()

    def flush(self, name):
        for eng in ENGS:
            assert not self.pending[eng], "unsignalled tail on " + eng
            wl = []
            for k, v in self.phase_dma[eng].items():
                if self.waited[eng].get(k, 0) < v:
                    self.waited[eng][k] = v
                    wl.append((self.semof(k), v))
            if wl:
                self.q[eng].append(lambda e, wl=wl: [e.wait_ge(s, v) for s, v in wl])
            self.phase_dma[eng] = {}
        q = self.q
        with self.nc.Block() as block:
            block.tensor(lambda e: [f(e) for f in q["pe"]])
            block.scalar(lambda e: [f(e) for f in q["act"]])
            block.vector(lambda e: [f(e) for f in q["dve"]])
            block.gpsimd(lambda e: [f(e) for f in q["pool"]])
            block.sync(lambda e: [f(e) for f in q["sp"]])
        self.q = {e: [] for e in ENGS}

    def mm(self, ob, out_ap, lhsT, rhs, reads, start=True, stop=True, sig=True):
        return self.op("pe", lambda e: e.matmul(out_ap, lhsT=lhsT, rhs=rhs, start=start, stop=stop,
                                               skip_group_check=True), reads=reads, writes=(ob,), sig=sig)

    def actf(self, ob, out_ap, in_ap, func, reads, scale=1.0, bias=0.0, accum=None, extra_w=()):
        kw = {}
        if accum is not None:
            kw["accum_out"] = accum
        return self.op("act", lambda e: e.activation(out=out_ap, in_=in_ap, func=func, bias=bias, scale=scale, **kw),
                       reads=reads, writes=(ob,) + tuple(extra_w))

    def tt(self, eng, ob, out_ap, a, b, opn, reads):
        return self.op(eng, lambda e: e.tensor_tensor(out=out_ap, in0=a, in1=b, op=opn), reads=reads, writes=(ob,))

    def stt(self, eng, ob, out_ap, a, sc, b, op0, op1, reads):
        return self.op(eng, lambda e: e.scalar_tensor_tensor(out=out_ap, in0=a, scalar=sc, in1=b, op0=op0, op1=op1),
                       reads=reads, writes=(ob,))

    def ts(self, eng, ob, out_ap, a, s1, s2, op0, op1, reads):
        if s2 is None:
            return self.op(eng, lambda e: e.tensor_scalar(out=out_ap, in0=a, scalar1=s1, scalar2=None, op0=op0),
                           reads=reads, writes=(ob,))
        return self.op(eng, lambda e: e.tensor_scalar(out=out_ap, in0=a, scalar1=s1, scalar2=s2, op0=op0, op1=op1),
                       reads=reads, writes=(ob,))

    def cp(self, eng, ob, out_ap, in_ap, reads):
        if eng == "act":
            return self.op("act", lambda e: e.copy(out=out_ap, in_=in_ap), reads=reads, writes=(ob,))
        return self.op(eng, lambda e: e.tensor_copy(out=out_ap, in_=in_ap), reads=reads, writes=(ob,))


def col(ap1d):
    return ap1d.rearrange("(p o) -> p o", o=1)


class Rot:
    def __init__(self, items):
        self.items = items
        self.i = 0

    def next(self):
        b = self.items[self.i % len(self.items)]
        self.i += 1
        return b


def build_program(L, SP, SS, taps=()):
    nc = bass.Bass("TRN2", target_bir_lowering=False)
    T = SP + SS
    NCH = T // 512

    def din(name, shape):
        return nc.dram_tensor(name, list(shape), F32, kind="ExternalInput").ap()

    I = {}
    I["x_p"] = din("x_p", [SP, D])
    I["x_s"] = din("x_s", [SS, D])
    I["c2"] = din("c2", [2, D])
    wshapes = dict(ln_g=[L, D], w_ada=[L, D, 3 * D], b_ada=[L, 3 * D], w_in=[L, D, DIN], q_a_norm=[L, 512],
                   w_q_up=[L, 512, 1536], kv_a_norm=[L, 256], w_kv_up=[L, 256, 2048], mla_qn=[L, 192],
                   mla_kn=[L, 192], ax_qn=[L, 128], ax_kn=[L, 128], win_qn=[L, 128], win_kn=[L, 128],
                   win_sink=[L, 8], rel_bias=[32, 8], w_br_mla=[L, 1024, D], w_br_ax=[L, 1024, D],
                   w_br_win=[L, 1024, D], w_out=[L, D, D])
    for k, s in wshapes.items():
        I[k] = din(k, s)
    I["k_ident"] = din("k_ident", [128, 128])
    I["k_rmat"] = din("k_rmat", [128, 128])
    I["k_cos_ax"] = din("k_cos_ax", [128, SP])
    I["k_sin_ax"] = din("k_sin_ax", [128, SP])
    I["k_cos_m"] = din("k_cos_m", [64, SP])
    I["k_sin_m"] = din("k_sin_m", [64, SP])
    I["k_widx"] = din("k_widx", [128, 384])
    I["k_wmask"] = din("k_wmask", [128, 384])
    y_p = nc.dram_tensor("y_p", [SP, D], F32, kind="ExternalOutput").ap()
    y_s = nc.dram_tensor("y_s", [SS, D], F32, kind="ExternalOutput").ap()

    def scr(name, shape, dt=BF16):
        kind = "ExternalOutput" if name in taps else "Internal"
        return nc.dram_tensor(name, list(shape), dt, kind=kind).ap()

    S = {}
    S["xs"] = scr("xs", [T, D], F32)
    S["modd"] = scr("modd", [L, 2, 3 * D], F32)
    S["btd"] = scr("btd", [128, 3 * 8 * 128], F32)
    S["hT"] = scr("hT", [D, T])
    S["cqnT"] = scr("cqnT", [512, T])
    S["ckvnT"] = scr("ckvnT", [256, T])
    S["krraw"] = scr("krraw", [64, T], F32)
    S["qnT"] = scr("qnT", [1024, T])
    S["qrT"] = scr("qrT", [512, T])
    S["knT"] = scr("knT", [1024, T])
    S["krT"] = scr("krT", [512, T])
    S["vm"] = scr("vm", [T, 1024])
    S["aqT"] = scr("aqT", [1024, T])
    S["akT"] = scr("akT", [256, T])
    S["wqT"] = scr("wqT", [1024, T])
    S["wkT"] = scr("wkT", [256, T])
    S["vaw"] = scr("vaw", [T, 512])
    S["gT"] = scr("gT", [3 * 1024, T])
    S["smT"] = scr("smT", [3 * 2048, T])
    S["AT"] = scr("AT", [3 * 1024, T])
    S["yT"] = scr("yT", [D, T])

    seqs = [(0, SP), (SP, SS)]

    def seq_of(t):
        return 0 if t < SP else 1

    def xsrc(l, t0, n):
        if l == 0:
            return I["x_p"][t0:t0 + n, :] if t0 < SP else I["x_s"][t0 - SP:t0 - SP + n, :]
        return S["xs"][t0:t0 + n, :]

    def xdst(l, t0, n):
        if l == L - 1:
            return y_p[t0:t0 + n, :] if t0 < SP else y_s[t0 - SP:t0 - SP + n, :]
        return S["xs"][t0:t0 + n, :]

    with ExitStack() as es:
        kb = KB(nc, es, L, SP, SS)
        ident = kb.sb("ident", [128, 128], BF16, dma=True, persist=True)
        rmat = kb.sb("rmat", [128, 128], BF16, dma=True, persist=True)
        ones = kb.sb("ones", [128, 128], BF16, persist=True)
        cT = kb.sb("cT", [128, 16, 128], BF16, persist=True)
        onesH = kb.sb("onesH", [128, 128], BF16, persist=True)
        gall = kb.sb("gall", [128, 16 * L], F32, dma=True, persist=True)
        rmatH = kb.sb("rmatH", [128, 128], BF16, persist=True)

        with kb.phase("init"):
            kb.dma("pool", ident[:], I["k_ident"], ident, True)
            kb.dma("pool", rmat[:], I["k_rmat"], rmat, True)
            kb.op("dve", lambda e: e.memset(ones[:], 1.0), writes=(ones,))
            kb.op("dve", lambda e: e.memset(onesH[:], 0.0), writes=(onesH,))
            kb.op("dve", lambda e: e.memset(onesH[0:64, :], 1.0), writes=(onesH,))
            kb.op("dve", lambda e: e.memset(rmatH[:], 0.0), writes=(rmatH,))
            kb.cp("dve", rmatH, rmatH[0:64, 0:64], rmat[0:64, 0:64], reads=(rmat,))
            kb.op("pool", lambda e: e.memset(gall[:], 1.0), writes=(gall,))
            for l_ in range(L):
                gspecs = [(I["q_a_norm"][l_, 0:128], 0, 128), (I["q_a_norm"][l_, 128:256], 1, 128),
                          (I["q_a_norm"][l_, 256:384], 2, 128), (I["q_a_norm"][l_, 384:512], 3, 128),
                          (I["kv_a_norm"][l_, 0:128], 4, 128), (I["kv_a_norm"][l_, 128:256], 5, 128),
                          (I["mla_qn"][l_, 0:128], 6, 128), (I["mla_qn"][l_, 128:192], 7, 64),
                          (I["mla_kn"][l_, 0:128], 8, 128), (I["mla_kn"][l_, 128:192], 9, 64),
                          (I["ax_qn"][l_, :], 10, 128), (I["ax_kn"][l_, :], 11, 128),
                          (I["win_qn"][l_, :], 12, 128), (I["win_kn"][l_, :], 13, 128)]
                for (src, c_, rows) in gspecs:
                    kb.dma("sp", gall[0:rows, 16 * l_ + c_:16 * l_ + c_ + 1], col(src), gall, True)
            cTf = kb.sb("cTf", [128, 16, 2], F32, dma=True)
            for kc in range(16):
                kb.dma("sp", cTf[:, kc, :], I["c2"][:, kc * 128:(kc + 1) * 128].rearrange("s p -> p s"), cTf, True, slow=True)
            kb.op("pool", lambda e: e.memset(cT[:], 0.0), writes=(cT,))
            kb.cp("dve", cT, cT[:, :, 0:2], cTf[:], reads=(cTf,))
            g_ = ada_gen(kb, 0, I, S, cT)
            for _ in g_:
                pass

        for l in range(L):
            if l == 0:
                phase_norm(kb, l, I, S, xsrc, ident, NCH, seq_of)
            phase_proj(kb, l, I, S, ident, rmat, (ones, onesH, rmatH, gall), NCH, seqs, seq_of)
            phase_dense_attn(kb, l, S, ident, seqs, mla=True, side=(bias_gen(kb, I, S) if l == 0 else None))
            phase_dense_attn(kb, l, S, ident, seqs, mla=False)
            phase_win_attn(kb, l, I, S, ident, seqs)
            phase_branch(kb, l, I, S, NCH, side=((lambda: ada_gen(kb, l + 1, I, S, cT)) if l + 1 < L else None))
            phase_out(kb, l, I, S, xsrc, xdst, NCH, seq_of, ident, fuse=(l < L - 1))
        print("recorded ops:", kb.nops)
    return nc


def phase_norm(kb, l, I, S, xsrc, ident, NCH, seq_of):
    with kb.phase("norm%d" % l):
        xt = Rot([kb.sb("xt%d" % i, [128, D], F32, dma=True) for i in range(3)])
        junk = kb.sb("junk", [128, D], BF16)
        ss = Rot([kb.sb("ss%d" % i, [128, 1], F32) for i in range(2)])
        lnv = Rot([kb.sb("lnv%d" % i, [128, 1], F32) for i in range(2)])
        rstd = Rot([kb.sb("rstd%d" % i, [128, 1], F32) for i in range(2)])
        t1 = Rot([kb.sb("t1_%d" % i, [128, D], F32) for i in range(2)])
        hb = Rot([kb.sb("hb%d" % i, [128, D], BF16) for i in range(2)])
        hTt = Rot([kb.sb("hTt%d" % i, [128, 16, 512], BF16, dma=True) for i in range(2)])
        gsb = kb.sb("gsb", [128, D], F32, dma=True)
        shb = kb.sb("shb", [128, D], F32, dma=True)
        pT = Rot([kb.pb("pT%d" % i, [128, 512], BF16) for i in range(4)])
        cur_seq = -1
        deferred = []
        for c in range(NCH):
            t0 = c * 512
            s = seq_of(t0)
            if s != cur_seq:
                cur_seq = s
                kb.dma("sp", shb[:], S["modd"][l, s, 0:D].partition_broadcast(128), shb, True)
                kb.dma("sp", gsb[:], S["modd"][l, s, D:2 * D].partition_broadcast(128), gsb, True)
            ht = hTt.next()
            for i in range(4):
                ti = c * 4 + i
                if ti == 0:
                    xq = [xt.next()]
                    kb.dma("sp", xq[0][:], xsrc(l, 0, 128), xq[0], True)
                x = xq.pop(0)
                if ti + 1 < NCH * 4:
                    xn_ = xt.next()
                    kb.dma("sp", xn_[:], xsrc(l, (ti + 1) * 128, 128), xn_, True)
                    xq.append(xn_)
                s_, ln_, r_, t_, h_ = ss.next(), lnv.next(), rstd.next(), t1.next(), hb.next()
                kb.actf(junk, junk[:], x[:], AF.Square, reads=(x,), accum=s_[:], extra_w=(s_,))
                kb.actf(ln_, ln_[:], s_[:], AF.Ln, reads=(s_,), scale=1.0 / D, bias=EPS)
                kb.actf(r_, r_[:], ln_[:], AF.Exp, reads=(ln_,), scale=-0.5)
                kb.stt("dve", t_, t_[:], x[:], r_[:, 0:1], gsb[:], ALU.mult, ALU.mult, reads=(x, r_, gsb))
                kb.tt("pool", h_, h_[:], t_[:], shb[:], ALU.add, reads=(t_, shb))
                for f in deferred:
                    f()
                del deferred[:]

                def fin(h_=h_, ht=ht, i=i, t0=t0):
                    for g in range(4):
                        p = pT.next()
                        for j in range(4):
                            kc = g * 4 + j
                            kb.op("pe", lambda e, p=p, j=j, h_=h_, kc=kc: e.transpose(
                                p[:, j * 128:(j + 1) * 128], h_[:, kc * 128:(kc + 1) * 128], ident[:]),
                                reads=(h_, ident), writes=(p,), sig=(j == 3))
                        eng = "act" if g % 2 == 0 else "dve"
                        kb.cp(eng, ht, ht[:, g * 4:(g + 1) * 4, i * 128:(i + 1) * 128],
                              p[:].rearrange("p (j t) -> p j t", j=4), reads=(p,))
                    if i == 3:
                        kb.dma("sp", S["hT"].rearrange("(kc p) t -> p kc t", p=128)[:, :, t0:t0 + 512], ht[:], ht, False)
                deferred.append(fin)
        for f in deferred:
            f()


def linear_fm(kb, name, in_dram, kcn, wgroups, NCH, seqs, seq_of, I, ident, rmat, ones, load_gains, kr_ctx=None):
    SP = kb.SP
    with kb.phase(name):
        ones, onesH, rmatH, gcol = ones
        gbase = 16 * load_gains

        def G(spec):
            return gcol[0:spec[2], gbase + spec[1]:gbase + spec[1] + 1]
        maxc = max(sum(n for _, n in g["srcs"]) for g in wgroups)
        wt_l = [kb.sb("wt%d" % i, [128, kcn, maxc + 64], BF16, dma=True) for i in range(2)]
        for v in wt_l:
            kb.op("pool", lambda e, v=v: e.memset(v[:, :, maxc:maxc + 64], 0.0), writes=(v,))
        wt = Rot(wt_l)
        xin = Rot([kb.sb("xin%d" % i, [128, kcn, 512], BF16, dma=True) for i in range(2)])
        raw = Rot([kb.sb("raw%d" % i, [128, 512], F32) for i in range(8)])
        sq_l = [kb.sb("sq%d" % i, [128, 512], BF16) for i in range(8)]
        for v in sq_l:
            kb.op("pool", lambda e, v=v: e.memset(v[:], 0.0), writes=(v,))
        sq = Rot(sq_l)
        lnv = Rot([kb.sb("lnv%d" % i, [128, 512], F32) for i in range(2)])
        rstd = Rot([kb.sb("rstd%d" % i, [128, 512], F32) for i in range(4)])
        xn = Rot([kb.sb("xn%d" % i, [128, 512], F32) for i in range(4)])
        xb_l = [kb.sb("xb%d" % i, [128, 512], BF16) for i in range(4)]
        for v in xb_l:
            kb.op("pool", lambda e, v=v: e.memset(v[:], 0.0), writes=(v,))
        xb = Rot(xb_l)
        t1 = Rot([kb.sb("t1_%d" % i, [128, 512], F32) for i in range(2)])
        t2 = Rot([kb.sb("t2_%d" % i, [128, 512], F32) for i in range(2)])
        ob = Rot([kb.sb("ob%d" % i, [128, 512], BF16, dma=True) for i in range(4)])
        ob32 = Rot([kb.sb("ob32_%d" % i, [128, 512], F32, dma=True) for i in range(2)])
        tmc = max([j["ncols"] for g in wgroups for j in g["jobs"] if j["kind"] == "tm"] + [0])
        if tmc:
            obt = Rot([kb.sb("obt%d" % i, [128, tmc], BF16, dma=True) for i in range(3)])
        need_ax = any(j.get("tabs") == "ax" for g in wgroups for j in g["jobs"])
        need_m = any(j.get("tabs") == "m" for g in wgroups for j in g["jobs"]) or kr_ctx is not None
        if need_ax:
            cosa = Rot([kb.sb("cosa%d" % i, [128, 512], F32, dma=True) for i in range(3)])
            sina = Rot([kb.sb("sina%d" % i, [128, 512], F32, dma=True) for i in range(3)])
        if need_m:
            cosm = Rot([kb.sb("cosm%d" % i, [64, 512], F32, dma=True) for i in range(3)])
            sinm = Rot([kb.sb("sinm%d" % i, [64, 512], F32, dma=True) for i in range(3)])
        if kr_ctx is not None:
            krr = Rot([kb.sb("krr%d" % i, [64, 512], F32, dma=True) for i in range(2)])
            sqkr_l = [kb.sb("sqkr%d" % i, [128, 512], BF16) for i in range(2)]
            for v in sqkr_l:
                kb.op("pool", lambda e, v=v: e.memset(v[:], 0.0), writes=(v,))
            sqkr = Rot(sqkr_l)
            krg = Rot([kb.sb("krg%d" % i, [64, 512], F32) for i in range(2)])
            krgb_l = [kb.sb("krgb%d" % i, [128, 512], BF16) for i in range(2)]
            for v in krgb_l:
                kb.op("pool", lambda e, v=v: e.memset(v[:], 0.0), writes=(v,))
            krgb = Rot(krgb_l)
            krt = Rot([kb.sb("krt%d" % i, [64, 512], F32) for i in range(2)])
            krrope = Rot([kb.sb("krrope%d" % i, [64, 512], F32) for i in range(2)])
        pm = Rot([kb.pb("pm%d" % i, [128, 512]) for i in range(4)])
        pss = Rot([kb.pb("pss%d" % i, [128, 512]) for i in range(2)])
        prot = Rot([kb.pb("prot%d" % i, [128, 512]) for i in range(2)])
        pend = deque()

        def advance():
            for _ in range(len(pend)):
                g = pend.popleft()
                try:
                    next(g)
                    pend.append(g)
                except StopIteration:
                    pass

        def store(buf, rows, dst, t0):
            kb.dma("sp", dst[0:rows, t0:t0 + 512], buf[0:rows, :], buf, False)

        def job_gen(job, x, w, t0, tabs, krs):
            kind = job["kind"]
            if kind == "act":
                p = pm.next()
                off = job["off"]
                for kc in range(kcn):
                    kb.mm(p, p[:], w[:, kc, off:off + 128], x[:, kc, :], reads=(w, x), start=(kc == 0),
                          stop=(kc == kcn - 1), sig=(kc == kcn - 1))
                o = ob.next()
                kb.actf(o, o[:], p[:], job["func"], reads=(p,))
                store(o, 128, job["out"], t0)
                return
            if kind == "tm":
                off, ncols = job["off"], job["ncols"]
                for i in range(4):
                    o = obt.next()
                    for nb in range(ncols // 512):
                        p = pm.next()
                        for kc in range(kcn):
                            kb.mm(p, p[:], x[:, kc, i * 128:(i + 1) * 128], w[:, kc, off + nb * 512:off + (nb + 1) * 512],
                                  reads=(x, w), start=(kc == 0), stop=(kc == kcn - 1), sig=(kc == kcn - 1))
                        kb.cp("act" if nb % 2 == 0 else "dve", o, o[:, nb * 512:(nb + 1) * 512], p[:], reads=(p,))
                    for (dst, doff, n) in job["outs"]:
                        kb.dma("sp", dst[t0 + i * 128:t0 + (i + 1) * 128, :], o[:, doff:doff + n], o, False)
                return
            if kind == "raw32":
                p = pm.next()
                off, rows = job["off"], job["rows"]
                for kc in range(kcn):
                    kb.mm(p, p[0:rows, :], w[:, kc, off:off + rows], x[:, kc, :], reads=(w, x), start=(kc == 0),
                          stop=(kc == kcn - 1), sig=(kc == kcn - 1))
                o = ob32.next()
                kb.cp("act", o, o[0:rows, :], p[0:rows, :], reads=(p,))
                store(o, rows, job["out"], t0)
                return
            blocks = job["blocks"]
            ps_, raws, sqs = [], [], []
            for (off, rows, gain, rope, out) in blocks:
                p = pm.next()
                for kc in range(kcn):
                    kb.mm(p, p[:], w[:, kc, off:off + 128], x[:, kc, :], reads=(w, x), start=(kc == 0),
                          stop=(kc == kcn - 1), sig=(kc == kcn - 1))
                ps_.append(p)
            for bi, (off, rows, gain, rope, out) in enumerate(blocks):
                r_, s_ = raw.next(), sq.next()
                kb.cp("act", r_, r_[0:rows, :], ps_[bi][0:rows, :], reads=(ps_[bi],))
                kb.actf(s_, s_[0:rows, :], ps_[bi][0:rows, :], AF.Square, reads=(ps_[bi],))
                raws.append(r_)
                sqs.append(s_)
            yield
            if kcn <= 4:
                yield
            pz = pss.next()
            nsq = len(blocks) + (1 if job.get("shared_kr") else 0)
            for bi, (off, rows, gain, rope, out) in enumerate(blocks):
                kb.mm(pz, pz[:], (ones if rows == 128 else onesH)[:, :], sqs[bi][:, :], reads=(ones, onesH, sqs[bi]),
                      start=(bi == 0), stop=(bi == nsq - 1), sig=(bi == nsq - 1))
            if job.get("shared_kr"):
                kb.mm(pz, pz[:], onesH[:, :], krs["sq"][:, :], reads=(onesH, krs["sq"]), start=False, stop=True)
            ln_, rs_ = lnv.next(), rstd.next()
            kb.actf(ln_, ln_[:], pz[:], AF.Ln, reads=(pz,), scale=1.0 / job["d"], bias=EPS)
            kb.actf(rs_, rs_[:], ln_[:], AF.Exp, reads=(ln_,), scale=-0.5)
            ropes = []
            for bi, (off, rows, gain, rope, out) in enumerate(blocks):
                if not rope:
                    o = ob.next()
                    kb.stt("dve", o, o[0:rows, :], raws[bi][0:rows, :], G(gain), rs_[0:rows, :], ALU.mult, ALU.mult,
                           reads=(raws[bi], rs_, gcol))
                    store(o, rows, out, t0)
                else:
                    n_, b_ = xn.next(), xb.next()
                    kb.stt("dve", n_, n_[0:rows, :], raws[bi][0:rows, :], G(gain), rs_[0:rows, :], ALU.mult, ALU.mult,
                           reads=(raws[bi], rs_, gcol))
                    kb.cp("act", b_, b_[0:rows, :], n_[0:rows, :], reads=(n_,))
                    ropes.append((rows, n_, b_, out))
            if job.get("shared_kr"):
                o = ob.next()
                kb.tt("dve", o, o[0:64, :], krs["rope"][0:64, :], rs_[0:64, :], ALU.mult, reads=(krs["rope"], rs_))
                store(o, 64, job["kr_out"], t0)
            if not ropes:
                return
            yield
            yield
            if kcn <= 4:
                yield
            cs, sn = tabs[job["tabs"]]
            for (rows, n_, b_, out) in ropes:
                pr = prot.next()
                kb.mm(pr, pr[:], (rmat if rows == 128 else rmatH)[:, :], b_[:, :], reads=(rmat, rmatH, b_))
                a_, c_ = t1.next(), t2.next()
                kb.tt("dve", a_, a_[0:rows, :], n_[0:rows, :], cs[0:rows, :], ALU.mult, reads=(n_, cs))
                kb.tt("dve", c_, c_[0:rows, :], pr[0:rows, :], sn[0:rows, :], ALU.mult, reads=(pr, sn))
                o = ob.next()
                kb.tt("pool", o, o[0:rows, :], a_[0:rows, :], c_[0:rows, :], ALU.add, reads=(a_, c_))
                store(o, rows, out, t0)

        in_v = in_dram.rearrange("(kc p) t -> p kc t", p=128)

        def load_w(g):
            w = wt.next()
            o_ = 0
            for (src, n) in g["srcs"]:
                kb.dma("pool", w[:, :, o_:o_ + n], src.rearrange("(kc p) n -> p kc n", p=128), w, True)
                o_ += n
            return w

        def prefetch(g, c):
            t0 = c * 512
            s = seq_of(t0)
            p0 = t0 - seqs[s][0]
            d = {}
            x = xin.next()
            kb.dma("sp", x[:], in_v[:, :, t0:t0 + 512], x, True)
            d["x"] = x
            tabs = {}
            if need_ax and any(j.get("tabs") == "ax" for j in g["jobs"]):
                ca, sa = cosa.next(), sina.next()
                kb.dma("sp", ca[:], I["k_cos_ax"][:, p0:p0 + 512], ca, True)
                kb.dma("sp", sa[:], I["k_sin_ax"][:, p0:p0 + 512], sa, True)
                tabs["ax"] = (ca, sa)
            if need_m and (kr_ctx is not None or any(j.get("tabs") == "m" for j in g["jobs"])):
                cm, sm_ = cosm.next(), sinm.next()
                kb.dma("sp", cm[:], I["k_cos_m"][:, p0:p0 + 512], cm, True)
                kb.dma("sp", sm_[:], I["k_sin_m"][:, p0:p0 + 512], sm_, True)
                tabs["m"] = (cm, sm_)
            d["tabs"] = tabs
            if kr_ctx is not None:
                kr_ = krr.next()
                kb.dma("sp", kr_[:], kr_ctx["src"][:, t0:t0 + 512], kr_, True)
                d["kr"] = kr_
            return d

        steps = [(gi, c) for gi in range(len(wgroups)) for c in range(NCH)]
        wts = {0: load_w(wgroups[0])}
        nxt = prefetch(wgroups[0], 0)
        for si, (gi, c) in enumerate(steps):
            g = wgroups[gi]
            if c == 0 and gi + 1 < len(wgroups):
                wts[gi + 1] = load_w(wgroups[gi + 1])
            cur = nxt
            if si + 1 < len(steps):
                nxt = prefetch(wgroups[steps[si + 1][0]], steps[si + 1][1])
            w = wts[gi]
            t0 = c * 512
            x, tabs = cur["x"], cur["tabs"]
            krs = None
            if kr_ctx is not None and any(j.get("shared_kr") for j in g["jobs"]):
                kr_ = cur["kr"]
                sk_, kg_, kgb_, kt_, kro_ = sqkr.next(), krg.next(), krgb.next(), krt.next(), krrope.next()
                kb.actf(sk_, sk_[0:64, :], kr_[:], AF.Square, reads=(kr_,))
                kb.ts("dve", kg_, kg_[:], kr_[:], G(kr_ctx["gain"]), None, ALU.mult, None, reads=(kr_, gcol))
                kb.cp("act", kgb_, kgb_[0:64, :], kg_[:], reads=(kg_,))
                pr = prot.next()
                kb.mm(pr, pr[:], rmatH[:, :], kgb_[:, :], reads=(rmatH, kgb_))
                kb.tt("dve", kt_, kt_[:], kg_[:], tabs["m"][0][:], ALU.mult, reads=(kg_, tabs["m"][0]))
                kb.tt("dve", kro_, kro_[:], pr[0:64, :], tabs["m"][1][:], ALU.mult, reads=(pr, tabs["m"][1]))
                kb.tt("pool", kro_, kro_[:], kro_[:], kt_[:], ALU.add, reads=(kro_, kt_))
                krs = {"sq": sk_, "rope": kro_}
            for job in g["jobs"]:
                gen = job_gen(job, x, w, t0, tabs, krs)
                try:
                    next(gen)
                    alive = True
                except StopIteration:
                    alive = False
                advance()
                if alive:
                    pend.append(gen)
        while pend:
            advance()


def linear_tm(kb, name, in_dram, kcn, srcs, outs, NCH):
    with kb.phase(name):
        ncols = sum(n for _, n in srcs)
        w = kb.sb("wtm", [128, kcn, ncols], BF16, dma=True)
        xin = Rot([kb.sb("xin%d" % i, [128, kcn, 512], BF16, dma=True) for i in range(2)])
        ob = Rot([kb.sb("ob%d" % i, [128, ncols], BF16, dma=True) for i in range(3)])
        pm = Rot([kb.pb("pm%d" % i, [128, 512]) for i in range(4)])
        o_ = 0
        for (src, n) in srcs:
            kb.dma("pool", w[:, :, o_:o_ + n], src.rearrange("(kc p) n -> p kc n", p=128), w, True)
            o_ += n
        in_v = in_dram.rearrange("(kc p) t -> p kc t", p=128)
        def pf(c):
            x = xin.next()
            kb.dma("sp", x[:], in_v[:, :, c * 512:c * 512 + 512], x, True)
            return x
        nx = pf(0)
        for c in range(NCH):
            t0 = c * 512
            x = nx
            if c + 1 < NCH:
                nx = pf(c + 1)
            for i in range(4):
                o = ob.next()
                for nb in range(ncols // 512):
                    p = pm.next()
                    for kc in range(kcn):
                        kb.mm(p, p[:], x[:, kc, i * 128:(i + 1) * 128], w[:, kc, nb * 512:(nb + 1) * 512], reads=(x, w),
                              start=(kc == 0), stop=(kc == kcn - 1), sig=(kc == kcn - 1))
                    kb.cp("act" if nb % 2 == 0 else "dve", o, o[:, nb * 512:(nb + 1) * 512], p[:], reads=(p,))
                for (dst, off, n) in outs:
                    kb.dma("sp", dst[t0 + i * 128:t0 + (i + 1) * 128, :], o[:, off:off + n], o, False)


def phase_proj(kb, l, I, S, ident, rmat, ones, NCH, seqs, seq_of):
    T = kb.T
    win = I["w_in"][l]

    def load_gains(gcol):
        specs = [(I["q_a_norm"][l, 0:128], 0, 128), (I["q_a_norm"][l, 128:256], 1, 128), (I["q_a_norm"][l, 256:384], 2, 128),
                 (I["q_a_norm"][l, 384:512], 3, 128), (I["kv_a_norm"][l, 0:128], 4, 128), (I["kv_a_norm"][l, 128:256], 5, 128),
                 (I["mla_qn"][l, 0:128], 6, 128), (I["mla_qn"][l, 128:192], 7, 64), (I["mla_kn"][l, 0:128], 8, 128),
                 (I["mla_kn"][l, 128:192], 9, 64), (I["ax_qn"][l, :], 10, 128), (I["ax_kn"][l, :], 11, 128),
                 (I["win_qn"][l, :], 12, 128), (I["win_kn"][l, :], 13, 128)]
        for (src, c, rows) in specs:
            kb.dma("sp", gcol[0:rows, c:c + 1], col(src), gcol, True)

    def rows_of(dst, r0, n):
        return dst[r0:r0 + n, :]

    groups = []
    jobs_lat = [
        {"kind": "norm", "d": 512, "blocks": [(128 * i, 128, ("g", i, 128), False, rows_of(S["cqnT"], 128 * i, 128)) for i in range(4)]},
        {"kind": "norm", "d": 256, "blocks": [(512 + 128 * i, 128, ("g", 4 + i, 128), False, rows_of(S["ckvnT"], 128 * i, 128)) for i in range(2)]},
        {"kind": "raw32", "off": 768, "rows": 64, "out": S["krraw"]},
    ]
    groups.append({"srcs": [(win[:, 0:832], 832)], "jobs": jobs_lat})

    def act_group(c0, nblk, func, dst, r0):
        gs = []
        b = 0
        while b < nblk:
            n = min(8, nblk - b)
            gs.append({"srcs": [(win[:, c0 + 128 * b:c0 + 128 * (b + n)], 128 * n)],
                       "jobs": [{"kind": "act", "off": 128 * i, "func": func, "out": rows_of(dst, r0 + 128 * (b + i), 128)} for i in range(n)]})
            b += n
        return gs

    def head_group(c0, nh, gc, rope, dst):
        return {"srcs": [(win[:, c0:c0 + 128 * nh], 128 * nh)],
                "jobs": [{"kind": "norm", "d": 128, "tabs": "ax",
                          "blocks": [(128 * i, 128, ("g", gc, 128), rope, rows_of(dst, 128 * i, 128))]} for i in range(nh)]}

    groups += act_group(C_GM, 8, AF.Silu, S["gT"], 0)
    groups.append(head_group(C_AQ, 8, 10, True, S["aqT"]))
    gk = {"srcs": [(win[:, C_AK:C_AK + 256], 256), (win[:, C_WK:C_WK + 256], 256)], "jobs": []}
    for i in range(2):
        gk["jobs"].append({"kind": "norm", "d": 128, "tabs": "ax", "blocks": [(128 * i, 128, ("g", 11, 128), True, rows_of(S["akT"], 128 * i, 128))]})
    for i in range(2):
        gk["jobs"].append({"kind": "norm", "d": 128, "blocks": [(256 + 128 * i, 128, ("g", 13, 128), False, rows_of(S["wkT"], 128 * i, 128))]})
    groups.append(gk)
    groups += act_group(C_GA, 8, AF.Silu, S["gT"], 1024)
    groups.append(head_group(C_WQ, 8, 12, False, S["wqT"]))
    groups += act_group(C_GW, 8, AF.Silu, S["gT"], 2048)
    groups += act_group(C_MM, 16, AF.Sigmoid, S["smT"], 0)
    groups += act_group(C_MA, 16, AF.Sigmoid, S["smT"], 2048)
    groups += act_group(C_MW, 16, AF.Sigmoid, S["smT"], 4096)

    groups.insert(0, {"srcs": [(win[:, C_AV:C_AV + 256], 256), (win[:, C_WV:C_WV + 256], 256)],
                      "jobs": [{"kind": "tm", "off": 0, "ncols": 512, "outs": [(S["vaw"], 0, 512)]}]})
    linear_fm(kb, "projA%d" % l, S["hT"], 16, groups, NCH, seqs, seq_of, I, ident, rmat, ones, l, None)
    wq = I["w_q_up"][l]
    jobs_q = [{"kind": "norm", "d": 192, "tabs": "m",
               "blocks": [(192 * h, 128, ("g", 6, 128), False, rows_of(S["qnT"], 128 * h, 128)),
                          (192 * h + 128, 64, ("g", 7, 64), True, rows_of(S["qrT"], 64 * h, 64))]} for h in range(8)]
    linear_fm(kb, "projQ%d" % l, S["cqnT"], 4, [{"srcs": [(wq, 1536)], "jobs": jobs_q}], NCH, seqs, seq_of, I, ident, rmat,
            ones, l, None)
    wkv = I["w_kv_up"][l]
    srcs_k = [(wkv[:, 256 * h:256 * h + 128], 128) for h in range(8)]
    jobs_k = [{"kind": "norm", "d": 192, "shared_kr": True, "kr_out": rows_of(S["krT"], 64 * h, 64),
               "blocks": [(128 * h, 128, ("g", 8, 128), False, rows_of(S["knT"], 128 * h, 128))]} for h in range(8)]
    srcs_v = [(wkv[:, 256 * h + 128:256 * h + 256], 128) for h in range(8)]
    gv = {"srcs": srcs_v, "jobs": [{"kind": "tm", "off": 0, "ncols": 1024, "outs": [(S["vm"], 0, 1024)]}]}
    linear_fm(kb, "projK%d" % l, S["ckvnT"], 2, [{"srcs": srcs_k, "jobs": jobs_k}, gv], NCH, seqs, seq_of, I, ident, rmat,
              ones, l, {"src": S["krraw"], "gain": ("g", 9, 64)})


def phase_dense_attn(kb, l, S, ident, seqs, mla, side=None):
    name = ("mla%d" if mla else "ax%d") % l
    scale = (192.0 if mla else 128.0) ** -0.5
    br = 0 if mla else 1
    with kb.phase(name):
        Smax = max(s for _, s in seqs)
        nb = 2
        qn = Rot([kb.sb("qn%d" % i, [128, Smax], BF16, dma=True) for i in range(nb)])
        kn = Rot([kb.sb("kn%d" % i, [128, Smax], BF16, dma=True) for i in range(nb)])
        if mla:
            qr_l = [kb.sb("qr%d" % i, [128, Smax], BF16, dma=True) for i in range(nb)]
            kr_l = [kb.sb("kr%d" % i, [128, Smax], BF16, dma=True) for i in range(nb)]
            for v in qr_l + kr_l:
                kb.op("pool", lambda e, v=v: e.memset(v[64:128, :], 0.0), writes=(v,))
            qr, kr = Rot(qr_l), Rot(kr_l)
        va_l = [kb.sb("va%d" % i, [128, Smax // 128, 129], BF16, dma=True) for i in range(nb)]
        for v in va_l:
            kb.op("pool", lambda e, v=v: e.memset(v[:, :, 128:129], 1.0), writes=(v,))
        va = Rot(va_l)
        gt = Rot([kb.sb("gt%d" % i, [128, Smax], BF16, dma=True) for i in range(nb)])
        pt = Rot([kb.sb("pt%d" % i, [128, 512], BF16) for i in range(4)])
        rden = Rot([kb.sb("rden%d" % i, [128, 1], F32) for i in range(8)])
        on = Rot([kb.sb("on%d" % i, [128, 128], BF16) for i in range(8)])
        at = Rot([kb.sb("at%d" % i, [128, 512], BF16, dma=True) for i in range(2)])
        ps = Rot([kb.pb("ps%d" % i, [128, 512]) for i in range(3)])
        accs = Rot([[kb.pb("acc%d_%d" % (s_, i), [128, 512]) for i in range(2)] for s_ in range(2)])
        pT = kb.pb("pT", [128, 512], BF16)

        work = []
        for (tok0, Sq) in seqs:
            for h in range(8):
                work.append((tok0, Sq, h))
        loaded = {}

        def load(wi):
            tok0, Sq, h = work[wi]
            kvh = h if mla else h // 4
            d = {"qn": qn.next(), "kn": kn.next(), "va": va.next(), "gt": gt.next()}
            qsrc, ksrc = (S["qnT"], S["knT"]) if mla else (S["aqT"], S["akT"])
            kb.dma("sp", d["qn"][:, 0:Sq], qsrc[128 * h:128 * h + 128, tok0:tok0 + Sq], d["qn"], True)
            kb.dma("sp", d["kn"][:, 0:Sq], ksrc[128 * kvh:128 * kvh + 128, tok0:tok0 + Sq], d["kn"], True)
            if mla:
                d["qr"], d["kr"] = qr.next(), kr.next()
                kb.dma("sp", d["qr"][0:64, 0:Sq], S["qrT"][64 * h:64 * h + 64, tok0:tok0 + Sq], d["qr"], True)
                kb.dma("sp", d["kr"][0:64, 0:Sq], S["krT"][64 * h:64 * h + 64, tok0:tok0 + Sq], d["kr"], True)
                vsrc = S["vm"][tok0:tok0 + Sq, 128 * h:128 * h + 128]
            else:
                vsrc = S["vaw"][tok0:tok0 + Sq, 128 * kvh:128 * kvh + 128]
            kb.dma("sp", d["va"][:, 0:Sq // 128, 0:128], vsrc.rearrange("(c p) d -> p c d", p=128), d["va"], True)
            kb.dma("sp", d["gt"][:, 0:Sq], S["gT"][br * 1024 + 128 * h:br * 1024 + 128 * h + 128, tok0:tok0 + Sq], d["gt"], True)
            loaded[wi] = d

        deferred = []

        def run_deferred():
            for f in deferred:
                f()
            del deferred[:]

        LOOK = 2
        load(0)
        for wi, (tok0, Sq, h) in enumerate(work):
            run_deferred()
            if wi + 1 < len(work):
                load(wi + 1)
            d = loaded.pop(wi)
            nk = Sq // 128
            if side is not None:
                try:
                    next(side)
                except StopIteration:
                    side = None

            def qk(qc, kc, d=d):
                p = ps.next()
                ks = slice(kc * 128, kc * 128 + 128)
                qs = slice(qc * 512, qc * 512 + 512)
                kb.mm(p, p[:], d["kn"][:, ks], d["qn"][:, qs], reads=(d["kn"], d["qn"]), start=True, stop=not mla,
                      sig=not mla)
                if mla:
                    kb.mm(p, p[:], d["kr"][:, ks], d["qr"][:, qs], reads=(d["kr"], d["qr"]), start=False, stop=True)
                return p

            def pv(kc, p, acc, d=d, nk=nk):
                t = pt.next()
                kb.actf(t, t[:], p[:], AF.Exp, reads=(p,), scale=scale)
                for j in range(4):
                    ab = acc[j // 2]
                    o0 = 256 * (j % 2)
                    kb.mm(ab, ab[:, o0:o0 + 129], t[:, j * 128:(j + 1) * 128], d["va"][:, kc, :], reads=(t, d["va"]),
                          start=(kc == 0 and j % 2 == 0), stop=(kc == nk - 1), sig=(kc == nk - 1))

            its = [(qc, kc) for qc in range(Sq // 512) for kc in range(nk)]
            pq = deque()
            for idx in range(min(LOOK, len(its))):
                pq.append(qk(*its[idx]))
            acc = None
            for idx, (qc, kc) in enumerate(its):
                if idx + LOOK < len(its):
                    pq.append(qk(*its[idx + LOOK]))
                if kc == 0:
                    acc = accs.next()
                pv(kc, pq.popleft(), acc)
                if kc == 1:
                    run_deferred()
                if kc == nk - 1:
                    ons = []
                    for j in range(4):
                        ab = acc[j // 2]
                        o0 = 256 * (j % 2)
                        r_, o_ = rden.next(), on.next()
                        kb.op("dve", lambda e, r_=r_, ab=ab, o0=o0: e.reciprocal(out=r_[:], in_=ab[:, o0 + 128:o0 + 129]),
                              reads=(ab,), writes=(r_,))
                        kb.ts("dve", o_, o_[:], ab[:, o0:o0 + 128], r_[:, 0:1], None, ALU.mult, None, reads=(ab, r_))
                        ons.append(o_)

                    def fin(ons=ons, qc=qc, d=d, h=h, tok0=tok0):
                        for j in range(4):
                            kb.op("pe", lambda e, j=j, o_=ons[j]: e.transpose(pT[:, j * 128:(j + 1) * 128], o_[:], ident[:]),
                                  reads=(ons[j], ident), writes=(pT,), sig=(j == 3))
                        a = at.next()
                        kb.tt("dve", a, a[:], pT[:], d["gt"][:, qc * 512:qc * 512 + 512], ALU.mult, reads=(pT, d["gt"]))
                        kb.dma("sp", S["AT"][br * 1024 + 128 * h:br * 1024 + 128 * h + 128,
                                             tok0 + qc * 512:tok0 + qc * 512 + 512], a[:], a, False)
                    deferred.append(fin)
        run_deferred()
        if side is not None:
            for _ in side:
                pass


def bias_gen(kb, I, S):
    rbb = kb.sb("rbb", [128, 256], F32, dma=True)
    widx = kb.sb("widx", [128, 384], F32, dma=True)
    wmask = kb.sb("wmask", [128, 384], F32, dma=True)
    bt = kb.sb("btb", [128, 3, 8, 128], F32, dma=True)
    tmpb = [kb.sb("tmpb%d" % i, [128, 3, 128], F32) for i in range(2)]
    kb.dma("sp", rbb[:], I["rel_bias"].rearrange("b h -> (b h)").partition_broadcast(128), rbb, True)
    kb.dma("sp", widx[:], I["k_widx"], widx, True)
    kb.dma("sp", wmask[:], I["k_wmask"], wmask, True)
    idx3 = widx[:].rearrange("p (d q) -> p d q", d=3)
    msk3 = wmask[:].rearrange("p (d q) -> p d q", d=3)
    for h in range(8):
        kb.cp("pool", bt, bt[:, :, h, :], msk3, reads=(wmask,))
    yield
    for h in range(8):
        for b in range(32):
            tb = tmpb[(h * 32 + b) % 2]
            kb.ts("dve", tb, tb[:], idx3, float(b), rbb[:, b * 8 + h:b * 8 + h + 1], ALU.is_equal, ALU.mult,
                  reads=(widx, rbb))
            kb.tt("pool", bt, bt[:, :, h, :], bt[:, :, h, :], tb[:], ALU.add, reads=(tb, bt))
            if b % 16 == 15:
                yield
    kb.dma("sp", S["btd"], bt[:].rearrange("p d h q -> p (d h q)"), bt, False)
    yield


def ada_gen(kb, l, I, S, cT):
    wada = Rot([kb.sb("wada%d" % i, [128, 16, 512], BF16, dma=True) for i in range(2)])
    bb = Rot([kb.sb("adab%d" % i, [2, 512], F32, dma=True) for i in range(2)])
    lg = Rot([kb.sb("adag%d" % i, [2, 512], F32, dma=True) for i in range(2)])
    mr = Rot([kb.sb("adam%d" % i, [2, 512], F32, dma=True) for i in range(2)])
    pmod = kb.pb("pmod", [128, 512])

    def pf(j):
        w, b_ = wada.next(), bb.next()
        kb.dma("pool", w[:], I["w_ada"][l, :, j * 512:(j + 1) * 512].rearrange("(kc p) n -> p kc n", p=128), w, True)
        kb.dma("sp", b_[:], I["b_ada"][l, j * 512:(j + 1) * 512].partition_broadcast(2), b_, True)
        g_ = None
        if 4 <= j < 8:
            g_ = lg.next()
            kb.dma("sp", g_[:], I["ln_g"][l, (j - 4) * 512:(j - 3) * 512].partition_broadcast(2), g_, True)
        return w, b_, g_
    nx = pf(0)
    for j in range(12):
        w, b_, g_ = nx
        if j + 1 < 12:
            nx = pf(j + 1)
        for kc in range(16):
            kb.mm(pmod, pmod[:], cT[:, kc, :], w[:, kc, :], reads=(cT, w), start=(kc == 0), stop=(kc == 15), sig=(kc == 15))
        m_ = mr.next()
        kb.tt("dve", m_, m_[:], pmod[0:2, :], b_[:], ALU.add, reads=(pmod, b_))
        if g_ is not None:
            kb.stt("dve", m_, m_[:], m_[:], 1.0, g_[:], ALU.add, ALU.mult, reads=(m_, g_))
        kb.dma("sp", S["modd"][l, :, j * 512:(j + 1) * 512], m_[:], m_, False)
        yield


def phase_win_attn(kb, l, I, S, ident, seqs):
    scale = 128.0 ** -0.5
    with kb.phase("win%d" % l):
        Smax = max(s for _, s in seqs)
        NB2 = 2
        qw = Rot([kb.sb("qw%d" % i, [128, 4, Smax], BF16, dma=True) for i in range(NB2)])
        gw = Rot([kb.sb("gw%d" % i, [128, 4, Smax], BF16, dma=True) for i in range(1)])
        kw = Rot([kb.sb("kw%d" % i, [128, Smax], BF16, dma=True) for i in range(NB2)])
        va_l = [kb.sb("va%d" % i, [128, Smax // 128, 129], BF16, dma=True) for i in range(NB2)]
        for v in va_l:
            kb.op("pool", lambda e, v=v: e.memset(v[:, :, 128:129], 1.0), writes=(v,))
        va = Rot(va_l)
        bt = kb.sb("bt", [128, 3, 8, 128], F32, dma=True)
        kb.dma("sp", bt[:].rearrange("p d h q -> p (d h q)"), S["btd"], bt, True)
        sk = kb.sb("sk", [128, 8], F32, dma=True)
        esk = kb.sb("esk", [128, 8], F32)
        kb.dma("sp", sk[:], I["win_sink"][l, :].partition_broadcast(128), sk, True)
        kb.actf(esk, esk[:], sk[:], AF.Exp, reads=(sk,))
        tt_ = Rot([kb.sb("tt%d" % i, [128, 512], F32) for i in range(4)])
        pt = Rot([kb.sb("pt%d" % i, [128, 512], BF16) for i in range(6)])
        den = Rot([kb.sb("den%d" % i, [128, 1], F32) for i in range(8)])
        rden = Rot([kb.sb("rden%d" % i, [128, 1], F32) for i in range(8)])
        on = Rot([kb.sb("on%d" % i, [128, 128], BF16) for i in range(8)])
        at = Rot([kb.sb("at%d" % i, [128, 4, 128], BF16, dma=True) for i in range(3)])
        ps = Rot([kb.pb("ps%d" % i, [128, 512]) for i in range(3)])
        accs = Rot([[kb.pb("acc%d_%d" % (s_, i), [128, 512]) for i in range(2)] for s_ in range(2)])
        pT = kb.pb("pT", [128, 512], BF16)

        units = []
        for (tok0, Sq) in seqs:
            for j in range(2):
                units.append((tok0, Sq, j))

        def load(u):
            tok0, Sq, j = u
            d = {"qw": qw.next(), "kw": kw.next(), "va": va.next()}
            kb.dma("sp", d["qw"][:, :, 0:Sq], S["wqT"][512 * j:512 * j + 512, tok0:tok0 + Sq].rearrange("(g p) t -> p g t", p=128), d["qw"], True)
            kb.dma("sp", d["kw"][:, 0:Sq], S["wkT"][128 * j:128 * j + 128, tok0:tok0 + Sq], d["kw"], True)
            kb.dma("sp", d["va"][:, 0:Sq // 128, 0:128], S["vaw"][tok0:tok0 + Sq, 256 + 128 * j:256 + 128 * j + 128].rearrange("(c p) d -> p c d", p=128), d["va"], True)
            return d

        def qblock(d, tok0, Sq, j, n):
            nblk = Sq // 128
            cs = [c for c in (n - 1, n, n + 1) if 0 <= c < nblk]
            es = []
            for c in cs:
                dd = c - n + 1
                p = ps.next()
                kb.mm(p, p[:].rearrange("p (g q) -> p g q", g=4), d["kw"][:, c * 128:(c + 1) * 128],
                      d["qw"][:, :, n * 128:(n + 1) * 128], reads=(d["kw"], d["qw"]))
                t_ = tt_.next()
                kb.stt("dve", t_, t_[:].rearrange("p (g q) -> p g q", g=4), p[:].rearrange("p (g q) -> p g q", g=4),
                       scale, bt[:, dd, 4 * j:4 * j + 4, :], ALU.mult, ALU.add, reads=(p, bt))
                e_ = pt.next()
                kb.actf(e_, e_[:], t_[:], AF.Exp, reads=(t_,))
                es.append((c, e_))
            yield
            acc = accs.next()
            for ci, (c, e_) in enumerate(es):
                for g in range(4):
                    ab = acc[g // 2]
                    o0 = 256 * (g % 2)
                    kb.mm(ab, ab[:, o0:o0 + 129], e_[:, g * 128:(g + 1) * 128], d["va"][:, c, :], reads=(e_, d["va"]),
                          start=(ci == 0 and g % 2 == 0), stop=(ci == len(es) - 1), sig=(ci == len(es) - 1))
            ons = []
            for g in range(4):
                ab = acc[g // 2]
                o0 = 256 * (g % 2)
                dn, r_, o_ = den.next(), rden.next(), on.next()
                h = 4 * j + g
                kb.tt("dve", dn, dn[:], ab[:, o0 + 128:o0 + 129], esk[:, h:h + 1], ALU.add, reads=(ab, esk))
                kb.op("dve", lambda e, r_=r_, dn=dn: e.reciprocal(out=r_[:], in_=dn[:]), reads=(dn,), writes=(r_,))
                kb.ts("dve", o_, o_[:], ab[:, o0:o0 + 128], r_[:, 0:1], None, ALU.mult, None, reads=(ab, r_))
                ons.append(o_)
            yield
            for g in range(4):
                kb.op("pe", lambda e, g=g, o_=ons[g]: e.transpose(pT[:, g * 128:(g + 1) * 128], o_[:], ident[:]),
                      reads=(ons[g], ident), writes=(pT,), sig=(g == 3))
            a = at.next()
            kb.tt("dve", a, a[:], pT[:].rearrange("p (g q) -> p g q", g=4), d["gw"][:, :, n * 128:(n + 1) * 128], ALU.mult,
                  reads=(pT, d["gw"]))
            kb.dma("sp", S["AT"][2048 + 512 * j:2048 + 512 * j + 512, tok0 + n * 128:tok0 + (n + 1) * 128].rearrange("(g p) t -> p g t", p=128),
                   a[:], a, False)

        pend = deque()

        def advance():
            for _ in range(len(pend)):
                g_ = pend.popleft()
                try:
                    next(g_)
                    pend.append(g_)
                except StopIteration:
                    pass

        nxt = load(units[0])
        for ui, (tok0, Sq, j) in enumerate(units):
            while pend:
                advance()
            d = nxt
            d["gw"] = gw.next()
            kb.dma("sp", d["gw"][:, :, 0:Sq], S["gT"][2048 + 512 * j:2048 + 512 * j + 512, tok0:tok0 + Sq].rearrange("(g p) t -> p g t", p=128), d["gw"], True)
            if ui + 1 < len(units):
                nxt = load(units[ui + 1])
            for n in range(Sq // 128):
                g_ = qblock(d, tok0, Sq, j, n)
                next(g_)
                advance()
                pend.append(g_)
        while pend:
            advance()


def phase_branch(kb, l, I, S, NCH, side=None):
    wbr = [I["w_br_mla"][l], I["w_br_ax"][l], I["w_br_win"][l]]
    with kb.phase("branch%d" % l):
        wt = Rot([[kb.sb("wb%d_%d" % (i, b), [128, 8, 512], BF16, dma=True) for b in range(3)] for i in range(2)])
        ain = Rot([[kb.sb("ain%d_%d" % (i, b), [128, 8, 512], BF16, dma=True) for b in range(3)] for i in range(2)])
        smi = Rot([[kb.sb("smi%d_%d" % (i, b), [128, 4, 512], BF16, dma=True) for b in range(3)] for i in range(2)])
        tq = Rot([[kb.sb("tq%d_%d" % (i, b), [128, 512], F32) for b in range(3)] for i in range(2)])
        u = Rot([kb.sb("u%d" % i, [128, 512], F32) for i in range(2)])
        yb = Rot([kb.sb("yb%d" % i, [128, 512], BF16, dma=True) for i in range(3)])
        pm = Rot([[kb.pb("pm%d_%d" % (i, b), [128, 512]) for b in range(3)] for i in range(2)])
        def load_w(ng):
            w = wt.next()
            for b in range(3):
                kb.dma("pool", w[b][:], wbr[b][:, ng * 512:(ng + 1) * 512].rearrange("(kc p) n -> p kc n", p=128), w[b], True)
            return w

        def pf(ng, c):
            t0 = c * 512
            a, sm = ain.next(), smi.next()
            for b in range(3):
                kb.dma("sp", a[b][:], S["AT"][1024 * b:1024 * (b + 1), t0:t0 + 512].rearrange("(kc p) t -> p kc t", p=128), a[b], True)
                kb.dma("sp", sm[b][:], S["smT"][2048 * b + 512 * ng:2048 * b + 512 * (ng + 1), t0:t0 + 512].rearrange("(k p) t -> p k t", p=128), sm[b], True)
            return a, sm

        steps = [(ng, c) for ng in range(4) for c in range(NCH)]
        wts = {0: load_w(0)}
        nx = pf(0, 0)
        sgen = side() if side is not None else None
        for si, (ng, c) in enumerate(steps):
            if c == 0 and ng + 1 < 4:
                wts[ng + 1] = load_w(ng + 1)
            if sgen is not None and si % 3 == 1:
                try:
                    next(sgen)
                except StopIteration:
                    sgen = None
            a, sm = nx
            if si + 1 < len(steps):
                nx = pf(*steps[si + 1])
            w = wts[ng]
            t0 = c * 512
            for blk in range(4):
                p, t_ = pm.next(), tq.next()
                for b in range(3):
                    for kc in range(8):
                        kb.mm(p[b], p[b][:], w[b][:, kc, blk * 128:(blk + 1) * 128], a[b][:, kc, :], reads=(w[b], a[b]),
                              start=(kc == 0), stop=(kc == 7), sig=(kc == 7))
                for b in range(3):
                    kb.tt("dve", t_[b], t_[b][:], p[b][:], sm[b][:, blk, :], ALU.mult, reads=(p[b], sm[b]))
                u_, y_ = u.next(), yb.next()
                kb.tt("pool", u_, u_[:], t_[0][:], t_[1][:], ALU.add, reads=(t_[0], t_[1]))
                kb.tt("pool", y_, y_[:], u_[:], t_[2][:], ALU.add, reads=(u_, t_[2]))
                r0 = (ng * 4 + blk) * 128
                kb.dma("sp", S["yT"][r0:r0 + 128, t0:t0 + 512], y_[:], y_, False)
        if sgen is not None:
            for _ in sgen:
                pass


def phase_out(kb, l, I, S, xsrc, xdst, NCH, seq_of, ident, fuse):
    with kb.phase("out%d" % l):
        wq_ = [kb.sb("wout%d" % q, [128, 16, 512], BF16, dma=True) for q in range(4)]
        for q in range(4):
            kb.dma("pool", wq_[q][:], I["w_out"][l][:, q * 512:(q + 1) * 512].rearrange("(kc p) n -> p kc n", p=128), wq_[q], True)
        gtb = kb.sb("gtb", [128, D], F32, dma=True)
        yin = Rot([kb.sb("yin%d" % i, [128, 16, 512], BF16, dma=True) for i in range(2)])
        xt = Rot([kb.sb("xt%d" % i, [128, D], F32, dma=True) for i in range(2)])
        xo = Rot([kb.sb("xo%d" % i, [128, D], F32, dma=True) for i in range(2)])
        tg = Rot([kb.sb("tg%d" % i, [128, 512], F32) for i in range(3)])
        pm = Rot([kb.pb("pm%d" % i, [128, 512]) for i in range(4)])
        if fuse:
            gsb = kb.sb("gsb", [128, D], F32, dma=True)
            shb = kb.sb("shb", [128, D], F32, dma=True)
            junk = kb.sb("junk", [128, D], BF16)
            ss = Rot([kb.sb("ss%d" % i, [128, 1], F32) for i in range(2)])
            lnv = Rot([kb.sb("lnv%d" % i, [128, 1], F32) for i in range(2)])
            rstd = Rot([kb.sb("rstd%d" % i, [128, 1], F32) for i in range(2)])
            t1 = kb.sb("t1", [128, D], F32)
            hb = Rot([kb.sb("hb%d" % i, [128, D], BF16) for i in range(2)])
            hTt = Rot([kb.sb("hTt%d" % i, [128, 16, 128], BF16, dma=True) for i in range(2)])
            pT = Rot([kb.pb("pT%d" % i, [128, 512], BF16) for i in range(4)])

        def pfy(c):
            y = yin.next()
            kb.dma("sp", y[:], S["yT"].rearrange("(kc p) t -> p kc t", p=128)[:, :, c * 512:c * 512 + 512], y, True)
            return y

        def pfx(ti):
            x = xt.next()
            kb.dma("sp", x[:], xsrc(l, ti * 128, 128), x, True)
            return x
        ny = pfy(0)
        nxx = pfx(0)
        cur = -1
        deferred = []
        for c in range(NCH):
            t0 = c * 512
            sq_ = seq_of(t0)
            if sq_ != cur:
                cur = sq_
                kb.dma("sp", gtb[:], S["modd"][l, sq_, 2 * D:3 * D].partition_broadcast(128), gtb, True)
                if fuse:
                    kb.dma("sp", shb[:], S["modd"][l + 1, sq_, 0:D].partition_broadcast(128), shb, True)
                    kb.dma("sp", gsb[:], S["modd"][l + 1, sq_, D:2 * D].partition_broadcast(128), gsb, True)
            y = ny
            if c + 1 < NCH:
                ny = pfy(c + 1)
            for i in range(4):
                ti = c * 4 + i
                x, o = nxx, xo.next()
                if ti + 1 < NCH * 4:
                    nxx = pfx(ti + 1)
                for nb in range(4):
                    p = pm.next()
                    for kc in range(16):
                        kb.mm(p, p[:], y[:, kc, i * 128:(i + 1) * 128], wq_[nb][:, kc, :], reads=(y, wq_[nb]),
                              start=(kc == 0), stop=(kc == 15), sig=(kc == 15))
                    t_ = tg.next()
                    ns = slice(nb * 512, (nb + 1) * 512)
                    kb.tt("dve", t_, t_[:], p[:], gtb[:, ns], ALU.mult, reads=(p, gtb))
                    kb.tt("pool", o, o[:, ns], t_[:], x[:, ns], ALU.add, reads=(t_, x))
                for f in deferred:
                    f()
                del deferred[:]
                kb.dma("sp", xdst(l, t0 + i * 128, 128), o[:], o, False)
                if fuse:
                    s_, ln_, r_, h_ = ss.next(), lnv.next(), rstd.next(), hb.next()
                    kb.actf(junk, junk[:], o[:], AF.Square, reads=(o,), accum=s_[:], extra_w=(s_,))
                    kb.actf(ln_, ln_[:], s_[:], AF.Ln, reads=(s_,), scale=1.0 / D, bias=EPS)
                    kb.actf(r_, r_[:], ln_[:], AF.Exp, reads=(ln_,), scale=-0.5)
                    kb.stt("dve", t1, t1[:], o[:], r_[:, 0:1], gsb[:], ALU.mult, ALU.mult, reads=(o, r_, gsb))
                    kb.tt("pool", h_, h_[:], t1[:], shb[:], ALU.add, reads=(t1, shb))

                    def fin(h_=h_, tt0=t0 + i * 128):
                        ht = hTt.next()
                        for g in range(4):
                            p = pT.next()
                            for j in range(4):
                                kc = g * 4 + j
                                kb.op("pe", lambda e, p=p, j=j, h_=h_, kc=kc: e.transpose(
                                    p[:, j * 128:(j + 1) * 128], h_[:, kc * 128:(kc + 1) * 128], ident[:]),
                                    reads=(h_, ident), writes=(p,), sig=(j == 3))
                            kb.cp("act" if g % 2 == 0 else "dve", ht, ht[:, g * 4:(g + 1) * 4, :],
                                  p[:].rearrange("p (j t) -> p j t", j=4), reads=(p,))
                        kb.dma("sp", S["hT"].rearrange("(kc p) t -> p kc t", p=128)[:, :, tt0:tt0 + 128], ht[:], ht, False)
                    deferred.append(fin)
        for f in deferred:
            f()


def _consts(SP):
    c = {}
    c["k_ident"] = np.eye(128, dtype=np.float32)
    R = np.zeros((128, 128), np.float32)
    for m in range(128):
        if m % 64 < 32:
            R[m + 32, m] = -1.0
        else:
            R[m - 32, m] = 1.0
    c["k_rmat"] = R
    half = 32
    freqs = (np.float32(10000.0) ** (-(np.arange(half, dtype=np.float32) / np.float32(half)))).astype(np.float32)
    pos = np.arange(SP, dtype=np.float32)
    row = np.floor(pos / 64).astype(np.float32)
    colp = (pos - row * 64).astype(np.float32)
    f128 = freqs[np.arange(128) % 32]
    posax = np.where((np.arange(128) < 64)[:, None], row[None, :], colp[None, :]).astype(np.float32)
    ang_ax = (posax * f128[:, None]).astype(np.float32)
    c["k_cos_ax"] = np.cos(ang_ax.astype(np.float64)).astype(np.float32)
    c["k_sin_ax"] = np.sin(ang_ax.astype(np.float64)).astype(np.float32)
    ang_m = (pos[None, :] * f128[:64, None]).astype(np.float32)
    c["k_cos_m"] = np.cos(ang_m.astype(np.float64)).astype(np.float32)
    c["k_sin_m"] = np.sin(ang_m.astype(np.float64)).astype(np.float32)
    k = np.arange(128)[:, None, None]
    d = np.arange(3)[None, :, None]
    q = np.arange(128)[None, None, :]
    rel = (d - 1) * 128 + k - q
    n = np.abs(rel)
    nf = np.maximum(n, 1).astype(np.float32)
    large = 8 + (np.log(nf / np.float32(8.0)).astype(np.float32) / np.float32(np.log(16.0)) * np.float32(8.0)).astype(np.float32).astype(np.int32)
    large = np.minimum(large, 15)
    bucket = np.where(rel > 0, 16, 0) + np.where(n < 8, n, large)
    valid = n <= 128
    c["k_widx"] = np.where(valid, bucket, -1).astype(np.float32).reshape(128, 384)
    c["k_wmask"] = np.where(valid, 0.0, NEGM).astype(np.float32).reshape(128, 384)
    return c


_WNAMES = ["ln_g", "w_ada", "b_ada", "w_in", "q_a_norm", "w_q_up", "kv_a_norm", "w_kv_up", "mla_qn", "mla_kn", "ax_qn",
           "ax_kn", "win_qn", "win_kn", "win_sink", "rel_bias", "w_br_mla", "w_br_ax", "w_br_win", "w_out"]


def run(inputs, n_cores, L, SP, SS, taps=()):
    nc = build_program(L, SP, SS, taps)
    consts = _consts(SP)
    shared = {k: np.ascontiguousarray(inputs[k], dtype=np.float32) for k in _WNAMES}
    shared.update(consts)
    in_maps = []
    for b in range(n_cores):
        m = dict(shared)
        m["x_p"] = np.ascontiguousarray(inputs["x_prompt"][b])
        m["x_s"] = np.ascontiguousarray(inputs["x_sample"][b])
        m["c2"] = np.ascontiguousarray(np.stack([inputs["c_prompt"][b], inputs["c_sample"][b]]))
        in_maps.append(m)
    res = run_bass_kernel_spmd(nc, in_maps, core_ids=list(range(n_cores)))
    return res


def kernel(**inputs):
    inputs = {k: np.asarray(v) for k, v in inputs.items()}
    res = run(inputs, 8, 4, 4096, 2048)
    y_p = np.stack([r["y_p"] for r in res.results]).astype(np.float32)
    y_s = np.stack([r["y_s"] for r in res.results]).astype(np.float32)
    return (y_p, y_s)
```
